# Optimizing a Trainium2 kernel written in Bass

```python
import jax, jax.numpy as jnp
from jax import lax
import numpy as np

D_MODEL = 1024
BATCH = 8
SEQ = 2048
DEPTH = 1

N_META = 16
D_SSD = D_MODEL
D_CONF = D_MODEL
D_MIX = D_SSD + D_CONF
SSD_HEADDIM = 64
SSD_HEADS = D_SSD // SSD_HEADDIM
SSD_GROUPS = 4
SSD_HPG = SSD_HEADS // SSD_GROUPS
SSD_STATE = 128
SSD_CONV = 4
SSD_CHUNK = 128
D_XBC = D_SSD + 2 * SSD_GROUPS * SSD_STATE
CONF_KERNEL = 31
D_IN = D_SSD + D_XBC + SSD_HEADS + 2 * D_CONF
PEER_HEADS = 8
PEER_NKEYS = 128
PEER_EXPERTS = PEER_NKEYS * PEER_NKEYS
PEER_DKEY = 256
PEER_TOPK = 16
PEER_BLOCK = 256
EPS = 1e-5

kernel_name = "hymba_ssd_conformer_peer_block"


def rmsnorm(x, w):
    xf = x.astype(jnp.float32)
    y = xf * lax.rsqrt(jnp.mean(xf * xf, axis=-1, keepdims=True) + EPS)
    return (y * w.astype(jnp.float32)).astype(x.dtype)


def causal_depthwise_conv(x, w, b):
    k = w.shape[0]
    y = lax.conv_general_dilated(x, w[:, None, :].astype(x.dtype), window_strides=(1,),
                                 padding=[(k - 1, 0)],
                                 dimension_numbers=("NWC", "WIO", "NWC"),
                                 feature_group_count=x.shape[-1])
    return y + b.astype(x.dtype)


def segsum(a):
    t = a.shape[-1]
    cs = jnp.cumsum(a, axis=-1)
    diff = cs[..., :, None] - cs[..., None, :]
    mask = jnp.tril(jnp.ones((t, t), dtype=bool))
    return jnp.where(mask, diff, -jnp.inf)


def ssd_chunked(xh, dt, a_neg, bm, cm):
    b, lp = xh.shape[0], xh.shape[1]
    nc = lp // SSD_CHUNK
    xh = xh.reshape(b, nc, SSD_CHUNK, SSD_GROUPS, SSD_HPG, SSD_HEADDIM)
    dtc = dt.reshape(b, nc, SSD_CHUNK, SSD_GROUPS, SSD_HPG)
    xdt = xh * dtc[..., None]
    bm = bm.reshape(b, nc, SSD_CHUNK, SSD_GROUPS, SSD_STATE)
    cm = cm.reshape(b, nc, SSD_CHUNK, SSD_GROUPS, SSD_STATE)
    dta = jnp.moveaxis(dtc * a_neg.reshape(SSD_GROUPS, SSD_HPG), 2, -1)
    a_cs = jnp.cumsum(dta, axis=-1)
    lmat = jnp.exp(segsum(dta))
    cb = jnp.einsum("bclgn,bcsgn->bcgls", cm, bm)
    y_diag = jnp.einsum("bcgls,bcgrls,bcsgrp->bclgrp", cb, lmat, xdt)
    decay_states = jnp.exp(a_cs[..., -1:] - a_cs)
    states = jnp.einsum("bclgn,bcgrl,bclgrp->bcgrpn", bm, decay_states, xdt)
    chunk_decay = jnp.exp(a_cs[..., -1])

    def step(h, inp):
        s, d = inp
        return h * d[..., None, None] + s, h

    h0 = jnp.zeros((b, SSD_GROUPS, SSD_HPG, SSD_HEADDIM, SSD_STATE), states.dtype)
    _, prev = lax.scan(step, h0, (jnp.moveaxis(states, 1, 0), jnp.moveaxis(chunk_decay, 1, 0)))
    prev = jnp.moveaxis(prev, 0, 1)
    y_off = jnp.einsum("bclgn,bcgrpn,bcgrl->bclgrp", cm, prev, jnp.exp(a_cs))
    return (y_diag + y_off).reshape(b, lp, SSD_HEADS, SSD_HEADDIM)


def ssd_mixer(z, xbc, dt_raw, conv_w, conv_b, dt_bias, a_log, d_skip, norm_w):
    b, l = z.shape[0], z.shape[1]
    xbc = jax.nn.silu(causal_depthwise_conv(xbc, conv_w, conv_b))
    xs, bm, cm = jnp.split(xbc, [D_SSD, D_SSD + SSD_GROUPS * SSD_STATE], axis=-1)
    f32 = jnp.float32
    dt = jax.nn.softplus(dt_raw.astype(f32) + dt_bias.astype(f32))
    a_neg = -jnp.exp(a_log.astype(f32))
    xh = xs.astype(f32).reshape(b, l, SSD_HEADS, SSD_HEADDIM)
    bm = bm.astype(f32).reshape(b, l, SSD_GROUPS, SSD_STATE)
    cm = cm.astype(f32).reshape(b, l, SSD_GROUPS, SSD_STATE)
    pad = (-N_META) % SSD_CHUNK
    padw = lambda t: jnp.pad(t, [(0, 0), (pad, 0)] + [(0, 0)] * (t.ndim - 2))
    y = ssd_chunked(padw(xh), padw(dt), a_neg, padw(bm), padw(cm))[:, pad:]
    y = y + d_skip.astype(f32)[:, None] * xh
    y = y.reshape(b, l, D_SSD) * jax.nn.silu(z.astype(f32))
    yg = y.reshape(b, l, SSD_GROUPS, D_SSD // SSD_GROUPS)
    yg = yg * lax.rsqrt(jnp.mean(yg * yg, axis=-1, keepdims=True) + EPS)
    return (yg.reshape(b, l, D_SSD) * norm_w.astype(f32)).astype(z.dtype)


def conformer_conv(u, conv_w, conv_b, ln_g, ln_b):
    a, g = jnp.split(u, 2, axis=-1)
    h = causal_depthwise_conv(a * jax.nn.sigmoid(g), conv_w, conv_b)
    hf = h.astype(jnp.float32)
    mu = jnp.mean(hf, axis=-1, keepdims=True)
    var = jnp.mean(jnp.square(hf - mu), axis=-1, keepdims=True)
    hn = (hf - mu) * lax.rsqrt(var + EPS) * ln_g.astype(jnp.float32) + ln_b.astype(jnp.float32)
    return jax.nn.silu(hn).astype(u.dtype)


def peer_ffn(x, w_query, sub_keys_1, sub_keys_2, w_down, w_up):
    b, l, d = x.shape
    t = b * l
    xt = x.reshape(t, d)
    q = (xt @ w_query).reshape(t, PEER_HEADS, PEER_DKEY)
    q1, q2 = jnp.split(q, 2, axis=-1)
    s1 = jnp.einsum("thd,hkd->thk", q1, sub_keys_1).astype(jnp.float32)
    s2 = jnp.einsum("thd,hkd->thk", q2, sub_keys_2).astype(jnp.float32)
    v1, i1 = lax.top_k(s1, PEER_TOPK)
    v2, i2 = lax.top_k(s2, PEER_TOPK)
    cand = (v1[..., :, None] + v2[..., None, :]).reshape(t, PEER_HEADS, PEER_TOPK * PEER_TOPK)
    cand_idx = (i1[..., :, None] * PEER_NKEYS + i2[..., None, :]).reshape(t, PEER_HEADS, PEER_TOPK * PEER_TOPK)
    top_s, pos = lax.top_k(cand, PEER_TOPK)
    expert_idx = jnp.take_along_axis(cand_idx, pos, axis=-1)
    gate = jax.nn.softmax(top_s, axis=-1).astype(x.dtype)
    tp = -(-t // PEER_BLOCK) * PEER_BLOCK
    nb = tp // PEER_BLOCK
    xp = jnp.pad(xt, [(0, tp - t), (0, 0)]).reshape(nb, PEER_BLOCK, d)
    ip = jnp.pad(expert_idx, [(0, tp - t), (0, 0), (0, 0)]).reshape(nb, PEER_BLOCK, PEER_HEADS, PEER_TOPK)
    gp = jnp.pad(gate, [(0, tp - t), (0, 0), (0, 0)]).reshape(nb, PEER_BLOCK, PEER_HEADS, PEER_TOPK)

    def block(args):
        xb, ib, gb = args
        u = w_down[ib]
        act = jax.nn.gelu(jnp.einsum("td,thkd->thk", xb, u), approximate=False) * gb
        v = w_up[ib]
        return jnp.einsum("thk,thkd->td", act, v)

    y = lax.map(block, (xp, ip, gp))
    return y.reshape(tp, d)[:t].reshape(b, l, d)


def setup_inputs(seed: int = 0) -> dict:
    key = jax.random.key(seed)
    ks = jax.random.split(key, 24)
    nrm = lambda k, shape, s: jax.random.normal(k, shape, jnp.float32) * s
    L = DEPTH
    dt0 = jnp.exp(jax.random.uniform(ks[5], (L, SSD_HEADS), jnp.float32, np.log(1e-3), np.log(1e-1)))
    return {
        "x": nrm(ks[0], (BATCH, SEQ, D_MODEL), 1.0),
        "meta_tokens": nrm(ks[1], (N_META, D_MODEL), 1.0),
        "norm_mix_w": 1.0 + nrm(ks[2], (L, D_MODEL), 0.02),
        "w_in": nrm(ks[3], (L, D_MODEL, D_IN), D_MODEL ** -0.5),
        "ssd_conv_w": nrm(ks[4], (L, SSD_CONV, D_XBC), SSD_CONV ** -0.5),
        "ssd_conv_b": nrm(ks[6], (L, D_XBC), 0.01),
        "ssd_dt_bias": dt0 + jnp.log(-jnp.expm1(-dt0)),
        "ssd_A_log": jnp.log(jax.random.uniform(ks[7], (L, SSD_HEADS), jnp.float32, 1.0, 16.0)),
        "ssd_D": 1.0 + nrm(ks[8], (L, SSD_HEADS), 0.02),
        "ssd_norm_w": 1.0 + nrm(ks[9], (L, D_SSD), 0.02),
        "conf_conv_w": nrm(ks[10], (L, CONF_KERNEL, D_CONF), CONF_KERNEL ** -0.5),
        "conf_conv_b": nrm(ks[11], (L, D_CONF), 0.01),
        "conf_ln_g": 1.0 + nrm(ks[12], (L, D_CONF), 0.02),
        "conf_ln_b": nrm(ks[13], (L, D_CONF), 0.01),
        "w_out": nrm(ks[14], (L, D_MIX, D_MODEL), D_MIX ** -0.5),
        "norm_ffn_w": 1.0 + nrm(ks[15], (L, D_MODEL), 0.02),
        "peer_w_query": nrm(ks[16], (L, D_MODEL, PEER_HEADS * PEER_DKEY), D_MODEL ** -0.5),
        "peer_sub_keys_1": nrm(ks[17], (L, PEER_HEADS, PEER_NKEYS, PEER_DKEY // 2), (PEER_DKEY // 2) ** -0.5),
        "peer_sub_keys_2": nrm(ks[18], (L, PEER_HEADS, PEER_NKEYS, PEER_DKEY // 2), (PEER_DKEY // 2) ** -0.5),
        "peer_w_down": nrm(ks[19], (L, PEER_EXPERTS, D_MODEL), D_MODEL ** -0.5),
        "peer_w_up": nrm(ks[20], (L, PEER_EXPERTS, D_MODEL), 0.25),
        "norm_final_w": 1.0 + nrm(ks[21], (D_MODEL,), 0.02),
    }


def reference(x, meta_tokens, norm_mix_w, w_in, ssd_conv_w, ssd_conv_b, ssd_dt_bias, ssd_A_log, ssd_D,
              ssd_norm_w, conf_conv_w, conf_conv_b, conf_ln_g, conf_ln_b, w_out, norm_ffn_w,
              peer_w_query, peer_sub_keys_1, peer_sub_keys_2, peer_w_down, peer_w_up, norm_final_w):
    b = x.shape[0]
    meta = jnp.broadcast_to(meta_tokens.astype(x.dtype)[None], (b, N_META, D_MODEL))
    h = jnp.concatenate([meta, x], axis=1)
    for l in range(DEPTH):
        u = rmsnorm(h, norm_mix_w[l])
        proj = u @ w_in[l]
        z, xbc, dt_raw, conf_in = jnp.split(
            proj, [D_SSD, D_SSD + D_XBC, D_SSD + D_XBC + SSD_HEADS], axis=-1)
        y_ssd = ssd_mixer(z, xbc, dt_raw, ssd_conv_w[l], ssd_conv_b[l], ssd_dt_bias[l],
                          ssd_A_log[l], ssd_D[l], ssd_norm_w[l])
        y_conf = conformer_conv(conf_in, conf_conv_w[l], conf_conv_b[l], conf_ln_g[l], conf_ln_b[l])
        h = h + jnp.concatenate([y_ssd, y_conf], axis=-1) @ w_out[l]
        h = h + peer_ffn(rmsnorm(h, norm_ffn_w[l]), peer_w_query[l], peer_sub_keys_1[l],
                         peer_sub_keys_2[l], peer_w_down[l], peer_w_up[l])
    return rmsnorm(h, norm_final_w)[:, N_META:]
```

```python
import contextlib
import numpy as np
import concourse.bass as bass
import concourse.mybir as mybir
from concourse.bass_utils import run_bass_kernel_spmd

F32 = mybir.dt.float32
BF16 = mybir.dt.bfloat16
U32 = mybir.dt.uint32
I32 = mybir.dt.int32
AF = mybir.ActivationFunctionType
ALU = mybir.AluOpType
AX = mybir.AxisListType

D = 1024
SEQ = 2048
NTILE = 17
TP = NTILE * 128
D_IN = 5136
EPS = 1e-5
SAME_ENGINE_SYNC = True
SSD_STEPS = 99
S2CUT = 99
S2SKIP = ()

PO = {}
_o = 0
for _n, _w in [("nmw", 8), ("nfw", 8), ("snw", 8), ("scw", 64), ("scb", 16), ("ccw", 248), ("ccb", 8),
               ("clg", 8), ("clb", 8), ("Drep", 16), ("dtb", 16), ("alog", 16), ("dtmask", 1)]:
    PO[_n] = (_o, _w)
    _o += _w
NPAR = _o


class KB:
    def __init__(self, nc, es):
        self.nc = nc
        self.es = es
        self.eng = {"pe": nc.tensor, "act": nc.scalar, "dve": nc.vector, "pool": nc.gpsimd, "sp": nc.sync}
        self.sems = {}
        self.cnt = {}
        self.waited = {e: {} for e in self.eng}
        self.res = {}
        for e in ("pe", "act", "dve", "pool"):
            self.new_sem(e)
        self.n_ins = 0

    def new_sem(self, key):
        self.sems[key] = self.es.enter_context(self.nc.semaphore("s_" + key))
        self.cnt[key] = 0

    def _deps(self, reads, writes):
        d = {}

        def add(tok):
            k, v = tok
            if d.get(k, 0) < v:
                d[k] = v

        for k in reads:
            w = self.res.get(k)
            if w and w[0]:
                add(w[0])
        for k in writes:
            w = self.res.get(k)
            if w:
                if w[0]:
                    add(w[0])
                for tok in w[1].items():
                    add(tok)
        return d

    def _commit(self, reads, writes, tok):
        for k in reads:
            w = self.res.setdefault(k, [None, {}])
            if w[1].get(tok[0], 0) < tok[1]:
                w[1][tok[0]] = tok[1]
        for k in writes:
            self.res[k] = [tok, {}]

    @staticmethod
    def _norm(reads, writes):
        r2, w2 = [], []
        for k in reads:
            if k.startswith("bank"):
                k = k.split("q")[0]
                if k not in w2:
                    w2.append(k)
            else:
                r2.append(k)
        for k in writes:
            if k.startswith("bank"):
                k = k.split("q")[0]
            if k not in w2:
                w2.append(k)
        return r2, w2

    def _need(self, e, reads, writes):
        d = self._deps(reads, writes)
        need = []
        for k, v in d.items():
            if self.waited[e].get(k, 0) >= v:
                continue
            if k == e and (e == "pe" or not SAME_ENGINE_SYNC):
                continue
            need.append((k, v))
        return need

    def op(self, e, fn, reads=(), writes=()):
        reads, writes = self._norm(reads, writes)
        need = self._need(e, reads, writes)
        for k, v in need[:-1]:
            self.eng[e].wait_ge(self.sems[k], v)
            self.waited[e][k] = v
        ins = fn(self.eng[e])
        if need:
            k, v = need[-1]
            ins._wait_ge(self.sems[k], v)
            self.waited[e][k] = v
        self.cnt[e] += 1
        ins.then_inc(self.sems[e], 1)
        self._commit(reads, writes, (e, self.cnt[e]))
        self.n_ins += 1
        return ins

    def dma(self, out, in_, reads, writes, sem, q="sp"):
        if sem not in self.sems:
            self.new_sem(sem)
        reads, writes = self._norm(reads, writes)
        need = self._need(q, reads, writes)
        for k, v in need:
            self.eng[q].wait_ge(self.sems[k], v)
            self.waited[q][k] = v
        ins = self.eng[q].dma_start(out=out, in_=in_)
        self.cnt[sem] += 16
        ins.then_inc(self.sems[sem], 16)
        self._commit(reads, writes, (sem, self.cnt[sem]))
        self.n_ins += 1
        return ins

    def barrier(self):
        for e in self.eng:
            for k, v in self.cnt.items():
                if v > 0 and self.waited[e].get(k, 0) < v and k != e:
                    self.eng[e].wait_ge(self.sems[k], v)
                    self.waited[e][k] = v

    def wait_all(self, e, keys):
        need = self._need(e, keys, ())
        for k, v in need:
            self.eng[e].wait_ge(self.sems[k], v)
            self.waited[e][k] = v


class SB:
    def __init__(self, nc, base, cap, kb):
        self.nc = nc
        self.kb = kb
        self.base = base
        self.cap = cap
        self.blocks = {}
        self.uid = 0

    def alloc(self, name, shape, dtype):
        esz = {F32: 4, BF16: 2, U32: 4, I32: 4}[dtype]
        n = 1
        for s in shape[1:]:
            n *= s
        size = ((n * esz + 63) // 64) * 64
        used = sorted(self.blocks.values())
        off = self.base
        for (o, s) in used:
            if off + size <= o:
                break
            off = max(off, o + s)
        assert off + size <= self.cap, f"SBUF overflow allocating {name} ({size}B at {off}, cap {self.cap})"
        self.blocks[name] = (off, size)
        self.uid += 1
        t = self.nc.alloc_sbuf_tensor_at(f"{name}_{self.uid}", list(shape), dtype, offset=off)
        return t.ap()

    def free(self, *names):
        for n in names:
            del self.blocks[n]
        self.kb.barrier()


def build_program(dbg=(), upto=99, ssd_chunks=NTILE):
    nc = bass.Bass("TRN2", target_bir_lowering=False)
    dram = {}

    def din(name, shape, dt=F32):
        dram[name] = nc.dram_tensor(name, list(shape), dt, kind="ExternalInput").ap()
        return dram[name]

    xpad = din("xpad", [TP, D])
    w_in = din("w_in", [D, D_IN])
    w_out = din("w_out", [2 * D, D])
    w_q = din("w_q", [D, 2048])
    keysT = din("keysT", [128, 16, 128])
    wdT = din("wdT", [D, 16384])
    w_up = din("w_up", [16384, D])
    params = din("params", [128, NPAR])
    nfinal = din("nfinal", [D])
    out = nc.dram_tensor("out", [SEQ, D], F32, kind="ExternalOutput").ap()
    h2d = nc.dram_tensor("h2d", [SEQ, D], F32, kind=("ExternalOutput" if "h2" in dbg else "Internal")).ap()
    Gd = nc.dram_tensor("Gd", [16, 16, 128, 1024], BF16, kind=("ExternalOutput" if "G" in dbg else "Internal")).ap()
    bitsd = nc.dram_tensor("bitsd", [16, 2, 128, 8, 128], BF16, kind="Internal").ap()
    dbg_out = {}

    def dbg_tensor(name, shape, dt=F32):
        dbg_out[name] = nc.dram_tensor("dbg_" + name, list(shape), dt, kind="ExternalOutput").ap()
        return dbg_out[name]

    es = contextlib.ExitStack()
    with es:
        kb = KB(nc, es)
        sb = SB(nc, ((nc.sbuf_base + 63) // 64) * 64, (nc.sbuf_top // 64) * 64, kb)
        quads = [es.enter_context(nc.psum_tensor(f"quad{i}", [128, 2048], F32)) for i in range(2)]
        qd = [q_[:, :] for q_ in quads]
        bk = [qd[i // 4][:, (i % 4) * 512:(i % 4 + 1) * 512] for i in range(8)]

        def BK(i):
            return [f"bank{i}q{q}" for q in range(4)]

        def BKH(i, h):
            return [f"bank{i}q{2 * h}", f"bank{i}q{2 * h + 1}"]

        def BKQ(i, q):
            return [f"bank{i}q{q}"]

        def bank_bf(i):
            return bk[i].bitcast(BF16)

        par = sb.alloc("par", [128, NPAR], F32)
        kb.dma(par, params, [], ["par"], "ld_par")

        def P(name, a=None, b=None):
            o, w = PO[name]
            if a is None:
                return par[:, o:o + w]
            return par[:, o + a:o + (b if b is not None else a + 1)]

        coli = sb.alloc("coli", [128, 128], I32)
        pidi = sb.alloc("pidi", [128, 1], I32)
        colf = sb.alloc("colf", [128, 128], F32)
        pidf = sb.alloc("pidf", [128, 1], F32)
        identf = sb.alloc("identf", [128, 128], F32)
        identb = sb.alloc("identb", [128, 128], BF16)
        triU = sb.alloc("triU", [128, 128], F32)
        maskneg = sb.alloc("maskneg", [128, 128], BF16)
        onesf = sb.alloc("onesf", [128, 128], F32)
        iotab = sb.alloc("iotab", [128, 128], BF16)
        kb.op("pool", lambda e: e.iota(coli, pattern=[[1, 128]], base=0, channel_multiplier=0), [], ["coli"])
        kb.op("pool", lambda e: e.iota(pidi, pattern=[[0, 1]], base=0, channel_multiplier=1), [], ["pidi"])
        kb.op("dve", lambda e: e.tensor_copy(out=colf, in_=coli), ["coli"], ["colf"])
        kb.op("dve", lambda e: e.tensor_copy(out=pidf, in_=pidi), ["pidi"], ["pidf"])
        kb.op("dve", lambda e: e.tensor_scalar(out=identf, in0=colf, scalar1=pidf[:, 0:1], scalar2=None,
                                               op0=ALU.is_equal), ["colf", "pidf"], ["identf"])
        kb.op("dve", lambda e: e.tensor_copy(out=identb, in_=identf), ["identf"], ["identb"])
        kb.op("dve", lambda e: e.tensor_scalar(out=triU, in0=colf, scalar1=pidf[:, 0:1], scalar2=None,
                                               op0=ALU.is_ge), ["colf", "pidf"], ["triU"])
        kb.op("dve", lambda e: e.tensor_scalar(out=maskneg, in0=triU, scalar1=-1.0, scalar2=30000.0,
                                               op0=ALU.add, op1=ALU.mult), ["triU"], ["maskneg"])
        kb.op("dve", lambda e: e.memset(onesf, 1.0), [], ["onesf"])
        kb.op("dve", lambda e: e.tensor_copy(out=iotab, in_=colf), ["colf"], ["iotab"])

        cdiag = sb.alloc("cdiag", [128, 8, 31, 128], BF16)
        def build_cdiag(cc):
            for k in range(31):
                kb.op("pool", lambda e: e.tensor_scalar(out=cdiag[:, cc, k, :], in0=identf,
                                                        scalar1=P("ccw", cc * 31 + k), scalar2=1.0,
                                                        op0=ALU.mult, op1=ALU.mult),
                      ["identf", "par"], [f"cdiag{cc}"])
        stg = [sb.alloc(f"stg{i}", [128, 2048], F32) for i in range(2)]
        stg_i = [0]

        def cast_rows(dst, src, scale, key, kcn, n, eng="pool"):
            for _ in cast_rows_gen(dst, src, scale, key, kcn, n, eng):
                pass

        def cast_rows_gen(dst, src, scale, key, kcn, n, eng="pool"):
            per = max(1, 2048 // n)
            for k0 in range(0, kcn, per):
                k1 = min(kcn, k0 + per)
                s = stg_i[0] % 2
                stg_i[0] += 1
                sv = stg[s][:, 0:(k1 - k0) * n].rearrange("p (k n) -> p k n", n=n)
                kb.dma(sv, src[:, k0:k1, :], [], [f"stg{s}"], f"ld_stg{s}")
                if scale is None:
                    kb.op(eng, lambda e, sv=sv, k0=k0, k1=k1: e.tensor_copy(out=dst[:, k0:k1, :], in_=sv),
                          [f"stg{s}"], [key])
                else:
                    for k in range(k0, k1):
                        kb.op(eng, lambda e, sv=sv, k=k, k0=k0: e.tensor_scalar(
                            out=dst[:, k, :], in0=sv[:, k - k0, :], scalar1=scale[:, k:k + 1], scalar2=1.0,
                            op0=ALU.mult, op1=ALU.mult), [f"stg{s}", "par"], [key])
                yield

        w_in_v = w_in.rearrange("(k p) c -> p k c", p=128)

        def rstd_from_ss(dst, ss, n, keyr, keyw, lnexp=False):
            kb.op("dve", lambda e: e.tensor_scalar(out=dst, in0=ss, scalar1=1.0 / n, scalar2=EPS,
                                                   op0=ALU.mult, op1=ALU.add), keyr, keyw)
            if lnexp:
                kb.op("act", lambda e: e.activation(out=dst, in_=dst, func=AF.Ln), keyw, keyw)
                kb.op("act", lambda e: e.activation(out=dst, in_=dst, func=AF.Exp, scale=-0.5), keyw, keyw)
            else:
                kb.op("act", lambda e: e.activation(out=dst, in_=dst, func=AF.Sqrt), keyw, keyw)
                kb.op("dve", lambda e: e.reciprocal(out=dst, in_=dst), keyw, keyw)

        uT = sb.alloc("uT", [128, 8, TP], BF16)
        ssA = sb.alloc("ssA", [128, NTILE], F32)
        xin = [sb.alloc(f"xin{i}", [128, D], F32) for i in range(4)]
        ub = [sb.alloc(f"ub{i}", [128, D], BF16) for i in range(3)]
        junk = sb.alloc("junk", [128, D], F32)
        def a_stageA(t):
            sx = t % 4
            kb.dma(xin[sx], xpad[t * 128:(t + 1) * 128, :], [], [f"xin{sx}"], f"ld_xin{sx}")
            kb.op("act", lambda e: e.activation(out=junk, in_=xin[sx], func=AF.Square, accum_out=ssA[:, t:t + 1]),
                  [f"xin{sx}"], ["junk", f"ssA{t}"])

        def a_stageB(t):
            sx, s = t % 4, t % 3
            rstd_from_ss(ssA[:, t:t + 1], ssA[:, t:t + 1], D, [f"ssA{t}"], [f"ssA{t}"])
            kb.op("dve", lambda e: e.tensor_scalar(out=ub[s], in0=xin[sx], scalar1=ssA[:, t:t + 1], scalar2=None,
                                                   op0=ALU.mult), [f"xin{sx}", f"ssA{t}"], [f"ub{s}"])

        def a_stageC(t):
            s = t % 3
            pb = bank_bf(t % 2)
            for dc in range(8):
                kb.op("pe", lambda e: e.transpose(out=pb[:, dc * 128:(dc + 1) * 128],
                                                  in_=ub[s][:, dc * 128:(dc + 1) * 128], identity=identb),
                      [f"ub{s}", "identb"], BK(t % 2))
            kb.op("act", lambda e: e.activation(out=uT[:, :, t * 128:(t + 1) * 128],
                                                in_=pb.rearrange("p (k n) -> p k n", n=128), func=AF.Copy),
                  BK(t % 2), [f"uT{t}"])

        for step in range(NTILE + 2):
            if step < NTILE:
                a_stageA(step)
            if 1 <= step <= NTILE:
                a_stageB(step - 1)
            if step >= 2:
                a_stageC(step - 2)
        uT_keys = [f"uT{t}" for t in range(NTILE)]
        sb.free("xin0", "xin1", "xin2", "xin3", "ub0", "ub1", "ub2", "junk", "ssA")

        GW = 30 + TP
        glu = sb.alloc("glu", [128, 8, GW], BF16)
        kb.op("pool", lambda e: e.memset(glu[:, :, 0:30], 0.0), [], ["glupad"])
        wag = [sb.alloc(f"wag{i}", [128, 8, 256], BF16) for i in range(2)]
        sig = [sb.alloc(f"sig{i}", [128, 512], F32) for i in range(2)]
        CA0 = 3088
        blocks = [(i * 512, 512) for i in range(4)] + [(2048, 128)]
        bi = 0
        for cc in range(8):
            ws = cc % 2
            cast_rows(wag[ws][:, :, 0:128], w_in_v[:, :, CA0 + cc * 128:CA0 + (cc + 1) * 128], P("nmw"), f"wag{ws}", 8, 128)
            cast_rows(wag[ws][:, :, 128:256], w_in_v[:, :, CA0 + 1024 + cc * 128:CA0 + 1024 + (cc + 1) * 128], P("nmw"),
                      f"wag{ws}", 8, 128)
            if cc >= 1:
                build_cdiag(cc - 1)
            for (t0, n) in blocks:
                ba, bg = (2 * bi) % 8, (2 * bi + 1) % 8
                bi += 1
                tkeys = [f"uT{t}" for t in range(t0 // 128, (t0 + n) // 128)]
                for (bb, c0) in ((ba, 0), (bg, 128)):
                    for dc in range(8):
                        kb.op("pe", lambda e: e.matmul(out=bk[bb][:, 0:n], lhsT=wag[ws][:, dc, c0:c0 + 128],
                                                       rhs=uT[:, dc, t0:t0 + n], start=(dc == 0), stop=(dc == 7)),
                              [f"wag{ws}"] + tkeys, BK(bb))
                sg = sig[bi % 2]
                kb.op("act", lambda e: e.activation(out=sg[:, 0:n], in_=bk[bg][:, 0:n], func=AF.Sigmoid),
                      BK(bg), [f"sig{bi % 2}"])
                kb.op("dve", lambda e: e.tensor_tensor(out=glu[:, cc, 30 + t0:30 + t0 + n], in0=bk[ba][:, 0:n],
                                                       in1=sg[:, 0:n], op=ALU.mult),
                      BK(ba) + [f"sig{bi % 2}"], [f"glu{cc}"])
        build_cdiag(7)
        sb.free("wag0", "wag1", "sig0", "sig1")

        ycT = sb.alloc("ycT", [128, 8, SEQ], BF16)
        sb.free("stg0", "stg1")
        hbs = [sb.alloc(f"hb{i}", [128, 8, 256], F32) for i in range(2)]
        hsq = sb.alloc("hsq", [128, 2, 256], F32)
        mean = sb.alloc("mean", [128, 256], F32)
        var = sb.alloc("var", [128, 256], F32)
        cvi = [0]

        def conf_conv(tb):
            hb = hbs[tb % 2]
            hbn = f"hb{tb % 2}_"
            s1b, s2b = (6, 7) if tb % 2 == 0 else (4, 5)
            P0 = 128 + tb * 256
            pend = None
            for cc in range(8):
                bb = cvi[0] % 4
                cvi[0] += 1
                for k in range(31):
                    kb.op("pe", lambda e: e.matmul(out=bk[bb][:, 0:256], lhsT=cdiag[:, cc, k, :],
                                                   rhs=glu[:, cc, P0 + k:P0 + k + 256], start=(k == 0), stop=(k == 30)),
                          [f"cdiag{cc}", f"glu{cc}", "glupad"], BK(bb))
                kb.op("act", lambda e: e.activation(out=hb[:, cc, :], in_=bk[bb][:, 0:256], func=AF.Identity,
                                                    bias=P("ccb", cc)), BK(bb) + ["par"], [hbn + str(cc)])
                kb.op("act", lambda e: e.activation(out=hsq[:, cc % 2, :], in_=bk[bb][:, 0:256], func=AF.Square,
                                                    bias=P("ccb", cc)), BK(bb) + ["par"], [f"hsq{cc % 2}"])
                if pend is not None:
                    pc = pend
                    kb.op("pe", lambda e: e.matmul(out=bk[s2b][:, 0:256], lhsT=onesf, rhs=hsq[:, pc % 2, :],
                                                   start=(pc == 0), stop=False), ["onesf", f"hsq{pc % 2}"], BK(s2b))
                pend = cc
            kb.op("pe", lambda e: e.matmul(out=bk[s2b][:, 0:256], lhsT=onesf, rhs=hsq[:, 1, :],
                                           start=False, stop=True), ["onesf", "hsq1"], BK(s2b))

        def conf_tail(tb):
            hb = hbs[tb % 2]
            hbn = f"hb{tb % 2}_"
            s1b, s2b = (6, 7) if tb % 2 == 0 else (4, 5)
            mean, var = mvs[tb % 2]
            mk, vk = f"mean{tb % 2}", f"var{tb % 2}"
            for cc in range(8):
                kb.op("pe", lambda e: e.matmul(out=bk[s1b][:, 0:256], lhsT=onesf, rhs=hb[:, cc, :],
                                               start=(cc == 0), stop=(cc == 7)), ["onesf", hbn + str(cc)], BK(s1b))
            kb.op("dve", lambda e: e.tensor_scalar(out=mean, in0=bk[s1b][:, 0:256], scalar1=1.0 / D, scalar2=None,
                                                   op0=ALU.mult), BK(s1b), [mk])
            kb.op("dve", lambda e: e.tensor_tensor(out=var, in0=mean, in1=mean, op=ALU.mult), [mk], [vk])
            kb.op("dve", lambda e: e.scalar_tensor_tensor(out=var, in0=bk[s2b][:, 0:256], scalar=1.0 / D, in1=var,
                                                          op0=ALU.mult, op1=ALU.subtract), BK(s2b) + [vk], [vk])
            kb.op("dve", lambda e: e.tensor_scalar(out=var, in0=var, scalar1=EPS, scalar2=None, op0=ALU.add), [vk], [vk])
            kb.op("act", lambda e: e.activation(out=var, in_=var, func=AF.Sqrt), [vk], [vk])
            kb.op("dve", lambda e: e.reciprocal(out=var, in_=var), [vk], [vk])
            hkeys = [hbn + str(cc) for cc in range(8)]
            kb.op("dve", lambda e: e.tensor_tensor(out=hb, in0=hb, in1=mean.unsqueeze(1).broadcast_to([128, 8, 256]),
                                                   op=ALU.subtract), hkeys + [mk], hkeys)
            kb.op("dve", lambda e: e.tensor_tensor(out=hb, in0=hb, in1=var.unsqueeze(1).broadcast_to([128, 8, 256]),
                                                   op=ALU.mult), hkeys + [vk], hkeys)
            for cc in range(8):
                kb.op("act", lambda e: e.activation(out=ycT[:, cc, tb * 256:(tb + 1) * 256], in_=hb[:, cc, :],
                                                    func=AF.Silu, scale=P("clg", cc), bias=P("clb", cc)),
                      [hbn + str(cc), "par"], [f"ycT{tb}"])

        mvs = [(mean, var), (sb.alloc("mean1", [128, 256], F32), sb.alloc("var1", [128, 256], F32))]
        conf_conv(0)
        for tb in range(1, 8):
            conf_conv(tb)
            conf_tail(tb - 1)
        conf_tail(7)
        sb.free("glu", "cdiag", "hb0", "hb1", "hsq", "mean", "var", "mean1", "var1")
        stg[0] = sb.alloc("stg0", [128, 2048], F32)
        stg[1] = sb.alloc("stg1", [128, 2048], F32)

        if upto < 2:
            ssd_chunks = 0
        xT = sb.alloc("xT", [128, 8, TP], BF16)
        BT = sb.alloc("BT", [128, 4, TP], BF16)
        CT = sb.alloc("CT", [128, 4, TP], BF16)
        dt = sb.alloc("dt", [128, NTILE, 16], F32)
        pre = [sb.alloc(f"pre{i}", [128, 3 + TP], BF16) for i in range(2)]
        wx = [sb.alloc(f"wx{i}", [128, 8, 128], BF16) for i in range(2)]
        sdiag = [sb.alloc(f"sdiag{i}", [128, 4, 128], BF16) for i in range(2)]
        wdt = sb.alloc("wdt", [128, 8, 16], BF16)
        sp_t = sb.alloc("sp_t", [128, 4, 16], F32)
        for i in range(2):
            kb.op("pool", lambda e: e.memset(pre[i][:, 0:3], 0.0), [], [f"prepad{i}"])
        bi = 0
        for xc in range(16 if upto >= 2 else 0):
            s = xc % 2
            col0 = 1024 + xc * 128
            cast_rows(wx[s], w_in_v[:, :, col0:col0 + 128], P("nmw"), f"wx{s}", 8, 128)
            for k in range(4):
                kb.op("pool", lambda e: e.tensor_scalar(out=sdiag[s][:, k, :], in0=identf, scalar1=P("scw", xc * 4 + k),
                                                        scalar2=1.0, op0=ALU.mult, op1=ALU.mult),
                      ["identf", "par"], [f"sdiag{s}"])
            for (t0, n) in blocks:
                bb = bi % 8
                bi += 1
                tkeys = [f"uT{t}" for t in range(t0 // 128, (t0 + n) // 128)]
                for dc in range(8):
                    kb.op("pe", lambda e: e.matmul(out=bk[bb][:, 0:n], lhsT=wx[s][:, dc, :], rhs=uT[:, dc, t0:t0 + n],
                                                   start=(dc == 0), stop=(dc == 7)), [f"wx{s}"] + tkeys, BK(bb))
                kb.op("act", lambda e: e.activation(out=pre[s][:, 3 + t0:3 + t0 + n], in_=bk[bb][:, 0:n], func=AF.Copy),
                      BK(bb), [f"pre{s}"])
            if xc < 8:
                dst, dkey = xT[:, xc, :], "xT"
            elif xc < 12:
                dst, dkey = BT[:, xc - 8, :], "BT"
            else:
                dst, dkey = CT[:, xc - 12, :], "CT"
            for (t0, n) in blocks:
                bb = bi % 8
                bi += 1
                for k in range(4):
                    kb.op("pe", lambda e: e.matmul(out=bk[bb][:, 0:n], lhsT=sdiag[s][:, k, :],
                                                   rhs=pre[s][:, t0 + k:t0 + k + n], start=(k == 0), stop=(k == 3)),
                          [f"sdiag{s}", f"pre{s}", f"prepad{s}"], BK(bb))
                kb.op("act", lambda e: e.activation(out=dst[:, t0:t0 + n], in_=bk[bb][:, 0:n], func=AF.Silu,
                                                    bias=P("scb", xc)), BK(bb) + ["par"],
                      [f"{dkey}{t}" for t in range(t0 // 128, (t0 + n) // 128)])
        wz = sb.alloc("wz", [128, 8, 1024], BF16)
        cast_rows(wz, w_in_v[:, :, 0:1024], P("nmw"), "wz", 8, 1024)
        DI = sb.alloc("DI", [128, 16, 128], BF16)
        for h in range(16):
            kb.op("pool", lambda e: e.tensor_scalar(out=DI[:, h, :], in0=identf, scalar1=P("Drep", h), scalar2=1.0,
                                                    op0=ALU.mult, op1=ALU.mult), ["identf", "par"], ["DI"])
        aneg = sb.alloc("aneg", [128, 16], F32)
        kb.op("act", lambda e: e.activation(out=aneg, in_=P("alog"), func=AF.Exp), ["par"], ["aneg"])
        kb.op("dve", lambda e: e.tensor_scalar(out=aneg, in0=aneg, scalar1=-1.0, scalar2=None, op0=ALU.mult),
              ["aneg"], ["aneg"])
        cast_rows(wdt, w_in_v[:, :, 3072:3088], P("nmw"), "wdt", 8, 16)
        for t in range(NTILE if upto >= 3 else 0):
            bb = bi % 8
            bi += 1
            for dc in range(8):
                kb.op("pe", lambda e: e.matmul(out=bk[bb][:, 0:16], lhsT=uT[:, dc, t * 128:(t + 1) * 128], rhs=wdt[:, dc, :],
                                               start=(dc == 0), stop=(dc == 7)), ["wdt", f"uT{t}"], BK(bb))
            xx, ax, ee, rr = sp_t[:, 0, :], sp_t[:, 1, :], sp_t[:, 2, :], sp_t[:, 3, :]
            kb.op("dve", lambda e: e.tensor_tensor(out=xx, in0=bk[bb][:, 0:16], in1=P("dtb"), op=ALU.add),
                  BK(bb) + ["par"], ["sp_x"])
            kb.op("act", lambda e: e.activation(out=ax, in_=xx, func=AF.Abs), ["sp_x"], ["sp_a"])
            kb.op("act", lambda e: e.activation(out=ee, in_=ax, func=AF.Exp, scale=-1.0), ["sp_a"], ["sp_e"])
            kb.op("act", lambda e: e.activation(out=ee, in_=ee, func=AF.Ln, bias=1.0), ["sp_e"], ["sp_e"])
            kb.op("dve", lambda e: e.tensor_scalar(out=rr, in0=xx, scalar1=0.0, scalar2=None, op0=ALU.max),
                  ["sp_x"], ["sp_r"])
            if t == 0:
                kb.op("dve", lambda e: e.tensor_tensor(out=rr, in0=rr, in1=ee, op=ALU.add), ["sp_r", "sp_e"], ["sp_r"])
                kb.op("dve", lambda e: e.tensor_scalar(out=dt[:, t, :], in0=rr, scalar1=P("dtmask", 0), scalar2=None,
                                                       op0=ALU.mult), ["sp_r", "par"], [f"dt{t}"])
            else:
                kb.op("dve", lambda e: e.tensor_tensor(out=dt[:, t, :], in0=rr, in1=ee, op=ALU.add),
                      ["sp_r", "sp_e"], [f"dt{t}"])
        sb.free("pre0", "pre1", "wx0", "wx1", "sdiag0", "sdiag1", "wdt", "sp_t")

        sb.free("stg0", "stg1")
        ynT = xT
        xtok = [sb.alloc(f"xtok{i}", [128, 1024], BF16) for i in range(2)]
        state = sb.alloc("state", [128, 16, 64], F32)
        statebf = [sb.alloc(f"statebf{i}", [128, 16, 64], BF16) for i in range(2)]
        kb.op("dve", lambda e: e.memset(state, 0.0), [], ["state"])
        kb.op("dve", lambda e: e.memset(statebf[0], 0.0), [], ["statebf0"])
        sm = {n: sb.alloc("sm_" + n, [128, NTILE, 16], F32) for n in ("dta", "ncs", "ecs", "dec", "wdec", "cd")}
        cbT = [sb.alloc(f"cbT{i}", [128, 4, 128], F32) for i in range(2)]
        NS = 4
        rhsh = [sb.alloc(f"rhsh{i}", [128, 128], F32) for i in range(NS)]
        LT = [sb.alloc(f"LT{i}", [128, 128], F32) for i in range(NS)]
        MT = [sb.alloc(f"MT{i}", [128, 128], BF16) for i in range(NS)]
        Bdec = [sb.alloc(f"Bdec{i}", [128, 128], BF16) for i in range(NS)]
        szb = sb.alloc("szb", [128, 1024], F32)
        yo = sb.alloc("yo", [128, 16, 64], F32)
        ynb = sb.alloc("ynb", [128, 1024], BF16)
        stmp = ynb.bitcast(F32).rearrange("p (h q) -> p h q", q=64)
        yb = yo.rearrange("p h q -> p (h q)")
        ssg = sb.alloc("ssg", [128, 4], F32)
        junk2 = sb.alloc("junk2", [128, 256], F32)
        btok = [sb.alloc(f"btok{i}", [128, 512], BF16) for i in range(2)]
        cst = sb.alloc("cst", [128, NTILE, 32], F32)
        hslot = 0
        CSB_BANKS = [4, 5, 6]
        ssd_deferred = []
        def ssd_decay(c):
            dta = sm["dta"][:, c, :]
            kb.op("dve", lambda e: e.tensor_tensor(out=dta, in0=dt[:, c, :], in1=aneg, op=ALU.mult),
                  [f"dt{c}", "aneg"], [f"dta{c}"])
            kb.op("pe", lambda e: e.matmul(out=bk[5][:, 0:16], lhsT=triU, rhs=dta, start=True, stop=True),
                  ["triU", f"dta{c}"], BK(5))
            kb.op("pe", lambda e: e.matmul(out=bk[5][:, 16:32], lhsT=onesf, rhs=dta, start=True, stop=True),
                  ["onesf", f"dta{c}"], BK(5))
            kb.op("dve", lambda e: e.tensor_copy(out=cst[:, c, :], in_=bk[5][:, 0:32]), BK(5), [f"cst{c}"])
            csc, tot = cst[:, c, 0:16], cst[:, c, 16:32]
            ncs, ecs, dec, wdec, cd = (sm[n][:, c, :] for n in ("ncs", "ecs", "dec", "wdec", "cd"))
            kb.op("dve", lambda e: e.tensor_scalar(out=ncs, in0=csc, scalar1=-1.0, scalar2=None, op0=ALU.mult),
                  [f"cst{c}"], [f"ncs{c}"])
            kb.op("act", lambda e: e.activation(out=ecs, in_=csc, func=AF.Exp), [f"cst{c}"], [f"ecs{c}"])
            kb.op("dve", lambda e: e.tensor_tensor(out=dec, in0=tot, in1=csc, op=ALU.subtract), [f"cst{c}"], [f"dec{c}"])
            kb.op("act", lambda e: e.activation(out=dec, in_=dec, func=AF.Exp), [f"dec{c}"], [f"dec{c}"])
            kb.op("dve", lambda e: e.tensor_tensor(out=wdec, in0=dec, in1=dt[:, c, :], op=ALU.mult),
                  [f"dec{c}", f"dt{c}"], [f"wdec{c}"])
            kb.op("act", lambda e: e.activation(out=cd, in_=tot, func=AF.Exp), [f"cst{c}"], [f"cd{c}"])

        if (ssd_chunks if upto >= 4 else 0) > 0:
            ssd_decay(0)
        for c in range(ssd_chunks if upto >= 4 else 0):
            s = c % 2
            ck = slice(c * 128, (c + 1) * 128)
            xb7 = bank_bf(7)
            for dc in range(8):
                kb.op("pe", lambda e: e.transpose(out=xb7[:, dc * 128:(dc + 1) * 128], in_=xT[:, dc, ck], identity=identb),
                      [f"xT{c}", "identb"], BK(7))
            kb.op("act", lambda e: e.activation(out=xtok[s], in_=xb7, func=AF.Copy), BK(7), [f"xtok{s}"])
            b6 = bank_bf(6)
            for g in range(4):
                kb.op("pe", lambda e: e.transpose(out=b6[:, g * 128:(g + 1) * 128], in_=BT[:, g, ck],
                                                  identity=identb), [f"BT{c}", "identb"], BK(6))
            kb.op("act", lambda e: e.activation(out=btok[s], in_=b6[:, 0:512], func=AF.Copy), BK(6), [f"btok{s}"])
            dta = sm["dta"][:, c, :]
            csc, tot = cst[:, c, 0:16], cst[:, c, 16:32]
            ncs, ecs, dec, wdec, cd = (sm[n][:, c, :] for n in ("ncs", "ecs", "dec", "wdec", "cd"))
            if c >= 1:
                for half in range(2):
                    for dc in range(8):
                        kb.op("pe", lambda e: e.matmul(out=bk[2 + half][:, :], lhsT=uT[:, dc, ck],
                                                       rhs=wz[:, dc, half * 512:(half + 1) * 512],
                                                       start=(dc == 0), stop=(dc == 7)), ["wz", f"uT{c}"], BK(2 + half))
                    kb.op("act", lambda e: e.activation(out=szb[:, half * 512:(half + 1) * 512], in_=bk[2 + half][:, :],
                                                        func=AF.Silu), BK(2 + half), [f"szb{half}"])
            for g in range(4):
                kb.op("pe", lambda e: e.matmul(out=bk[5][:, g * 128:(g + 1) * 128], lhsT=BT[:, g, ck], rhs=CT[:, g, ck],
                                               start=True, stop=True), [f"BT{c}", f"CT{c}"], BK(5))
            kb.op("act", lambda e: e.activation(out=cbT[s], in_=bk[5].rearrange("p (g n) -> p g n", n=128), func=AF.Copy),
                  BK(5), [f"cbT{s}"])
            for h in range(16):
                g = h // 4
                kb.op("pe", lambda e: e.matmul(out=bk[h // 8][:, (h % 8) * 64:(h % 8 + 1) * 64], lhsT=CT[:, g, ck],
                                               rhs=statebf[s][:, h, :], start=True, stop=True),
                      [f"CT{c}", f"statebf{s}"], BK(h // 8))
            for half in range(2):
                kb.op("dve", lambda e: e.tensor_tensor(
                    out=yo[:, half * 8:(half + 1) * 8, :], in0=bk[half].rearrange("p (h q) -> p h q", q=64),
                    in1=ecs[:, half * 8:(half + 1) * 8].unsqueeze(2).broadcast_to([128, 8, 64]), op=ALU.mult),
                    BK(half) + [f"ecs{c}"], [f"yo{half}"])
            while ssd_deferred:
                ssd_deferred.pop(0)()
            LA = 3
            hbase = hslot
            hslot += 16

            def stage1(h):
                sl = (hbase + h) % NS
                cbank = CSB_BANKS[(hbase + h) % 3]
                kb.op("dve", lambda e: e.tensor_scalar(out=rhsh[sl], in0=triU, scalar1=dta[:, h:h + 1], scalar2=None,
                                                       op0=ALU.mult), ["triU", f"dta{c}"], [f"rhsh{sl}"])
                kb.op("pe", lambda e: e.matmul(out=bk[cbank][:, 0:128], lhsT=onesf, rhs=rhsh[sl],
                                               start=True, stop=False), ["onesf", f"rhsh{sl}"], BK(cbank))
                kb.op("pe", lambda e: e.matmul(out=bk[cbank][:, 0:128], lhsT=identb, rhs=maskneg,
                                               start=False, stop=True), ["identb", "maskneg"], BK(cbank))
                kb.op("act", lambda e: e.activation(out=LT[sl], in_=bk[cbank][:, 0:128], func=AF.Exp,
                                                    bias=ncs[:, h:h + 1]), BK(cbank) + [f"ncs{c}"], [f"LT{sl}"])

            def stage2(h):
                g = h // 4
                sl = (hbase + h) % NS
                kb.op("dve", lambda e: e.scalar_tensor_tensor(out=MT[sl], in0=LT[sl], scalar=dt[:, c, h:h + 1],
                                                              in1=cbT[s][:, g, :], op0=ALU.mult, op1=ALU.mult),
                      [f"LT{sl}", f"dt{c}", f"cbT{s}"], [f"MT{sl}"])
                ysl = bk[h // 8][:, (h % 8) * 64:(h % 8 + 1) * 64]
                xh = xtok[s][:, h * 64:(h + 1) * 64]
                kb.op("pe", lambda e: e.matmul(out=ysl, lhsT=MT[sl], rhs=xh, start=True, stop=False),
                      [f"MT{sl}", f"xtok{s}"], BK(h // 8))
                kb.op("pe", lambda e: e.matmul(out=ysl, lhsT=DI[:, h, :], rhs=xh, start=False, stop=True),
                      ["DI", f"xtok{s}"], BK(h // 8))
                kb.op("dve", lambda e: e.tensor_scalar(out=Bdec[sl], in0=btok[s][:, g * 128:(g + 1) * 128],
                                                       scalar1=wdec[:, h:h + 1], scalar2=None, op0=ALU.mult),
                      [f"btok{s}", f"wdec{c}"], [f"Bdec{sl}"])
                kb.op("pe", lambda e: e.matmul(out=bk[2 + h // 8][:, (h % 8) * 64:(h % 8 + 1) * 64], lhsT=Bdec[sl], rhs=xh,
                                               start=True, stop=True), [f"Bdec{sl}", f"xtok{s}"], BK(2 + h // 8))

            for step in range(16 + LA):
                if step < 16:
                    stage1(step)
                if step >= LA:
                    stage2(step - LA)
            if c + 1 < (ssd_chunks if upto >= 4 else 0):
                ssd_decay(c + 1)
            if c < NTILE - 1:
                for half in range(2):
                    hs = slice(half * 8, (half + 1) * 8)
                    kb.op("dve", lambda e: e.tensor_tensor(out=stmp, in0=state[:, hs, :],
                                                           in1=cd[:, hs].unsqueeze(2).broadcast_to([128, 8, 64]),
                                                           op=ALU.mult), ["state", f"cd{c}"], ["ynb"])
                    kb.op("dve", lambda e: e.tensor_tensor(out=state[:, hs, :], in0=stmp,
                                                           in1=bk[2 + half].rearrange("p (h q) -> p h q", q=64),
                                                           op=ALU.add), ["ynb"] + BK(2 + half), ["state"])
                kb.op("act", lambda e: e.activation(out=statebf[1 - s], in_=state, func=AF.Copy), ["state"],
                      [f"statebf{1 - s}"])
            if c >= 1:
                for half in range(2):
                    hs = slice(half * 512, (half + 1) * 512)
                    kb.op("dve", lambda e: e.tensor_tensor(out=yb[:, hs], in0=bk[half][:, :],
                                                           in1=yo[:, half * 8:(half + 1) * 8, :].rearrange("p h q -> p (h q)"),
                                                           op=ALU.add), BK(half) + [f"yo{half}"], [f"yo{half}"])
                    kb.op("dve", lambda e: e.tensor_tensor(out=yb[:, hs], in0=yb[:, hs], in1=szb[:, hs], op=ALU.mult),
                          [f"yo{half}", f"szb{half}"], [f"yo{half}"])
                for g in range(4):
                    kb.op("act", lambda e: e.activation(out=junk2, in_=yb[:, g * 256:(g + 1) * 256], func=AF.Square,
                                                        accum_out=ssg[:, g:g + 1]), [f"yo{g // 2}"], ["junk2", "ssg"])
                rstd_from_ss(ssg, ssg, 256, ["ssg"], ["ssg"], lnexp=True)
                kb.op("dve", lambda e: e.tensor_tensor(out=ynb.rearrange("p (g n) -> p g n", n=256),
                                                       in0=yb.rearrange("p (g n) -> p g n", n=256),
                                                       in1=ssg.unsqueeze(2).broadcast_to([128, 4, 256]), op=ALU.mult),
                      ["yo0", "yo1", "ssg"], ["ynb"])
                def yn_transposes(c=c):
                    xb7_ = bank_bf(7)
                    for dc in range(8):
                        kb.op("pe", lambda e: e.transpose(out=xb7_[:, dc * 128:(dc + 1) * 128],
                                                          in_=ynb[:, dc * 128:(dc + 1) * 128], identity=identb),
                              ["ynb", "identb"], BK(7))
                    kb.op("act", lambda e: e.activation(out=ynT[:, :, (c - 1) * 128:c * 128],
                                                        in_=xb7_.rearrange("p (k n) -> p k n", n=128), func=AF.Copy),
                          BK(7), [f"xT{c - 1}"])
                ssd_deferred.append(yn_transposes)
        while ssd_deferred:
            ssd_deferred.pop(0)()
        ynT_keys = [f"xT{r}" for r in range(16)]

        run_c = upto >= 5
        for n in ["uT", "BT", "CT", "dt", "wz", "DI", "aneg", "xtok0", "xtok1", "state", "statebf0", "statebf1",
                  "sm_dta", "sm_ncs", "sm_ecs", "sm_dec", "sm_wdec", "sm_cd", "cbT0", "cbT1", "szb", "yo", "ynb", "ssg",
                  "junk2", "btok0", "btok1", "cst"] + [f"{a}{i}" for a in ("rhsh", "LT", "MT", "Bdec") for i in range(NS)]:
            del sb.blocks[n]
        kb.barrier()
        stg[0] = sb.alloc("stg0", [128, 2048], F32)
        stg[1] = sb.alloc("stg1", [128, 2048], F32)
        xnT = sb.alloc("xnT", [128, 8, SEQ], BF16)
        if run_c:
            wo = sb.alloc("wo", [128, 16, 1024], BF16)
            wq = sb.alloc("wq", [128, 8, 2048], BF16)
            w_out_v = w_out.rearrange("(k p) c -> p k c", p=128)
            cast_rows(wo[:, 0:8, :], w_out_v[:, 0:8, :], P("snw"), "wo", 8, 1024, eng="dve")
            cast_rows(wo[:, 8:16, :], w_out_v[:, 8:16, :], None, "wo", 8, 1024, eng="dve")
            xin = [sb.alloc(f"xin{i}", [128, D], F32) for i in range(2)]
            w_q_v = w_q.rearrange("(k p) c -> p k c", p=128)
            for cb_ in range(8):
                cast_rows(wq[:, :, cb_ * 256:(cb_ + 1) * 256], w_q_v[:, :, cb_ * 256:(cb_ + 1) * 256], P("nfw"), "wq", 8, 256)
            keysTb = sb.alloc("keysTb", [128, 16, 128], BF16)
            cast_rows(keysTb, keysT, None, "keysTb", 16, 128)
            h2 = xin
            ub = [sb.alloc(f"ub{i}", [128, D], BF16) for i in range(2)]
            ssC = sb.alloc("ssC", [128, 16], F32)
            bi = 0
            c_pend = []

            def c_transposes(r):
                s = r % 2
                rk = slice(r * 128, (r + 1) * 128)
                pbi = 6 + r % 2
                pb = bank_bf(pbi)
                for dc in range(8):
                    kb.op("pe", lambda e: e.transpose(out=pb[:, dc * 128:(dc + 1) * 128],
                                                      in_=ub[s][:, dc * 128:(dc + 1) * 128], identity=identb),
                          [f"ub{s}", "identb"], BK(pbi))
                kb.op("act", lambda e: e.activation(out=xnT[:, :, rk], in_=pb.rearrange("p (k n) -> p k n", n=128),
                                                    func=AF.Copy), BK(pbi), [f"xnT{r}"])

            for r in range(16):
                s = r % 2
                rk = slice(r * 128, (r + 1) * 128)
                kb.dma(xin[s], xpad[(r + 1) * 128:(r + 2) * 128, :], [], [f"xin{s}"], f"ld_xin{s}")
                for half in range(2):
                    bb = bi % 6
                    bi += 1
                    for kc in range(16):
                        lt = ynT[:, kc, rk] if kc < 8 else ycT[:, kc - 8, rk]
                        lk = f"xT{r}" if kc < 8 else f"ycT{r // 2}"
                        kb.op("pe", lambda e: e.matmul(out=bk[bb][:, :], lhsT=lt, rhs=wo[:, kc, half * 512:(half + 1) * 512],
                                                       start=(kc == 0), stop=(kc == 15)), [lk, "wo"], BK(bb))
                    kb.op("dve", lambda e: e.tensor_tensor(out=h2[s][:, half * 512:(half + 1) * 512], in0=bk[bb][:, :],
                                                           in1=xin[s][:, half * 512:(half + 1) * 512], op=ALU.add),
                          BK(bb) + [f"xin{s}"], [f"xin{s}"])
                kb.dma(h2d[rk, :], h2[s], [f"xin{s}"], [f"h2d{r}"], f"st_h2{s}")
                kb.op("act", lambda e: e.activation(out=ub[s], in_=h2[s], func=AF.Square, accum_out=ssC[:, r:r + 1]),
                      [f"xin{s}"], [f"ub{s}", f"ssC{r}"])
                rstd_from_ss(ssC[:, r:r + 1], ssC[:, r:r + 1], D, [f"ssC{r}"], [f"ssC{r}"])
                kb.op("dve", lambda e: e.tensor_scalar(out=ub[s], in0=h2[s], scalar1=ssC[:, r:r + 1], scalar2=None,
                                                       op0=ALU.mult), [f"xin{s}", f"ssC{r}"], [f"ub{s}"])
                c_pend.append(r)
                if len(c_pend) > 1:
                    c_transposes(c_pend.pop(0))
            while c_pend:
                c_transposes(c_pend.pop(0))
            for n in ["wo", "xin0", "xin1", "ub0", "ub1"]:
                del sb.blocks[n]
        for n in ["xT", "ycT"]:
            del sb.blocks[n]
        kb.barrier()

        run_d = upto >= 6
        if run_d:
            if "wq" not in sb.blocks:
                wq = sb.alloc("wq", [128, 8, 2048], BF16)
                w_q_v = w_q.rearrange("(k p) c -> p k c", p=128)
                for cb_ in range(8):
                    cast_rows(wq[:, :, cb_ * 256:(cb_ + 1) * 256], w_q_v[:, :, cb_ * 256:(cb_ + 1) * 256], P("nfw"), "wq", 8, 256)
                keysTb = sb.alloc("keysTb", [128, 16, 128], BF16)
                cast_rows(keysTb, keysT, None, "keysTb", 16, 128)
            sb.free("stg0", "stg1")
            Gs = [sb.alloc("Gs0", [128, 128, 128], BF16)] * 2
            qT = sb.alloc("qT", [128, 16, 128], BF16)
            vv = sb.alloc("vv", [128, 16, 16], F32)
            idx = sb.alloc("idx", [128, 16, 16], U32)
            idxf = sb.alloc("idxf", [128, 16, 16], F32)
            cand = sb.alloc("cand", [128, 8, 256], F32)
            scr2 = sb.alloc("scr2", [128, 8, 256], F32)
            ts = sb.alloc("ts", [128, 8, 16], F32)
            pos = sb.alloc("pos", [128, 8, 16], U32)
            apu = sb.alloc("apu", [128, 128], U32)
            bpu = sb.alloc("bpu", [128, 128], U32)
            aposf = sb.alloc("aposf", [128, 128], F32)
            bposf = sb.alloc("bposf", [128, 128], F32)
            zs = sb.alloc("zs", [128, 8], F32)
            eq = sb.alloc("eq", [128, 128, 16], F32)
            sel3 = sb.alloc("sel3", [128, 3, 128], F32)
            BUV = sb.alloc("BUV", [128, 128 * 128], BF16)
            sgnU = sb.alloc("sgnU", [128, 128], BF16)
            sgnV = sb.alloc("sgnV", [128, 128], BF16)
            sgi = sb.alloc("sgi", [128, 128], I32)
            sgf = sb.alloc("sgf", [128, 128], F32)
            pm = sb.alloc("pm", [128, 2], I32)
            pmf = sb.alloc("pmf", [128, 8], F32)
            sbU = sb.alloc("sbU", [128, 8, 128], BF16)
            sbV = sb.alloc("sbV", [128, 8, 128], BF16)
            iu = sb.alloc("iu", [128, 2, 128], U32)
            bt = [sb.alloc(f"bt{i}", [128, 128], U32) for i in range(4)]
            g2 = sb.alloc("g2", [128, 128], F32)
            NUB = 3
            Ub = [sb.alloc(f"Ub{i}", [128, 8, 128], BF16) for i in range(NUB)]
            Vb = [sb.alloc(f"Vb{i}", [128, 8, 128], BF16) for i in range(NUB)]
            kb.op("dve", lambda e: e.tensor_single_scalar(out=pm[:, 0:1], in_=pidi, scalar=31, op=ALU.bitwise_and), ["pidi"], ["pm0"])
            kb.op("dve", lambda e: e.tensor_single_scalar(out=pm[:, 1:2], in_=pidi, scalar=7, op=ALU.bitwise_and), ["pidi"], ["pm1"])
            kb.op("dve", lambda e: e.tensor_tensor(out=sgi, in0=coli, in1=pm[:, 1:2].broadcast_to([128, 128]),
                                                   op=ALU.logical_shift_right), ["coli", "pm1"], ["sgi"])
            kb.op("dve", lambda e: e.tensor_single_scalar(out=sgi, in_=sgi, scalar=1, op=ALU.bitwise_and), ["sgi"], ["sgi"])
            kb.op("dve", lambda e: e.tensor_copy(out=pmf[:, 0:2], in_=pm[:, 0:2]), ["pm0", "pm1"], ["pmf"])
            kb.op("dve", lambda e: e.tensor_scalar(out=pmf[:, 2:3], in0=pmf[:, 1:2], scalar1=7.0, scalar2=-2.0,
                                                   op0=ALU.is_equal, op1=ALU.mult), ["pmf"], ["pmf2"])
            kb.op("dve", lambda e: e.tensor_scalar(out=pmf[:, 2:3], in0=pmf[:, 2:3], scalar1=1.0, scalar2=None, op0=ALU.add),
                  ["pmf2"], ["pmf2"])
            kb.op("dve", lambda e: e.tensor_scalar(out=pmf[:, 3:4], in0=pmf[:, 0:1], scalar1=8.0, scalar2=None, op0=ALU.is_lt),
                  ["pmf"], ["pmf3"])
            kb.op("dve", lambda e: e.tensor_scalar(out=pmf[:, 4:5], in0=pmf[:, 0:1], scalar1=16.0, scalar2=None, op0=ALU.is_lt),
                  ["pmf"], ["pmf4"])
            kb.op("dve", lambda e: e.tensor_tensor(out=pmf[:, 4:5], in0=pmf[:, 4:5], in1=pmf[:, 3:4], op=ALU.subtract),
                  ["pmf4", "pmf3"], ["pmf4"])
            kb.op("dve", lambda e: e.tensor_tensor(out=pmf[:, 3:5], in0=pmf[:, 3:5], in1=pmf[:, 2:3].broadcast_to([128, 2]),
                                                   op=ALU.mult), ["pmf3", "pmf4", "pmf2"], ["pmf34"])
            kb.op("dve", lambda e: e.tensor_scalar(out=sgf, in0=sgi, scalar1=2.0, scalar2=-1.0, op0=ALU.mult, op1=ALU.add),
                  ["sgi"], ["sgf"])
            kb.op("dve", lambda e: e.tensor_scalar(out=sgnU, in0=sgf, scalar1=pmf[:, 3:4], scalar2=None, op0=ALU.mult),
                  ["sgf", "pmf34"], ["sgnU"])
            kb.op("dve", lambda e: e.tensor_scalar(out=sgnV, in0=sgf, scalar1=pmf[:, 4:5], scalar2=None, op0=ALU.mult),
                  ["sgf", "pmf34"], ["sgnV"])
            kb.op("dve", lambda e: e.memset(sbU[:, 7, :], -6.0), [], ["sbU7"])
            v4 = vv.rearrange("p (h two) k -> p h two k", two=2)
            i4 = idxf.rearrange("p (h two) k -> p h two k", two=2)
            c4 = cand.rearrange("p h (a b) -> p h a b", b=16)
            eq4 = eq.rearrange("p (h k) a -> p h k a", k=16)
            gbi = 0
            scs = sb.alloc("scs", [128, 16, 128], F32)

            def burst(tt):
                tk = slice(tt * 128, (tt + 1) * 128)
                for hh in range(16):
                    bb = hh // 4
                    for dc in range(8):
                        kb.op("pe", lambda e: e.matmul(out=bk[bb][:, (hh % 4) * 128:(hh % 4 + 1) * 128],
                                                       lhsT=wq[:, dc, hh * 128:(hh + 1) * 128], rhs=xnT[:, dc, tk],
                                                       start=(dc == 0), stop=(dc == 7)), ["wq", f"xnT{tt}"], BK(bb))
                    if hh % 4 == 3:
                        kb.op("act", lambda e: e.activation(out=qT[:, bb * 4:(bb + 1) * 4, :],
                                                            in_=bk[bb].rearrange("p (c n) -> p c n", n=128), func=AF.Copy),
                              BK(bb), [f"qT{bb}"])
                for hh in range(16):
                    bb = 4 + hh // 4
                    kb.op("pe", lambda e: e.matmul(out=bk[bb][:, (hh % 4) * 128:(hh % 4 + 1) * 128], lhsT=qT[:, hh, :],
                                                   rhs=keysTb[:, hh, :], start=True, stop=True),
                          [f"qT{hh // 4}", "keysTb"], BK(bb))
                    if hh % 4 == 3:
                        kb.op("act", lambda e: e.activation(out=scs[:, hh - 3:hh + 1, :],
                                                            in_=bk[bb].rearrange("p (c n) -> p c n", n=128), func=AF.Copy),
                              BK(bb), [f"scs{h_}" for h_ in range(hh - 3, hh + 1)])

            def prepA(tt):
                svs = [scs[:, hh, :] for hh in range(16)]
                for hh in range(16):
                    kb.op("dve", lambda e: e.max(out=vv[:, hh, 0:8], in_=svs[hh]), [f"scs{hh}"], [f"vv{hh}"])
                for hh in range(16):
                    kb.op("dve", lambda e: e.max_index(out=idx[:, hh, 0:8], in_max=vv[:, hh, 0:8], in_values=svs[hh]),
                          [f"scs{hh}", f"vv{hh}"], [f"idx{hh}"])
                for hh in range(16):
                    kb.op("dve", lambda e: e.match_replace(out=scs[:, hh, :], in_to_replace=vv[:, hh, 0:8], in_values=svs[hh],
                                                           imm_value=-1e30), [f"vv{hh}"], [f"scs{hh}"])
                for hh in range(16):
                    kb.op("dve", lambda e: e.max(out=vv[:, hh, 8:16], in_=scs[:, hh, :]), [f"scs{hh}"], [f"vv{hh}"])
                for hh in range(16):
                    kb.op("dve", lambda e: e.max_index(out=idx[:, hh, 8:16], in_max=vv[:, hh, 8:16], in_values=scs[:, hh, :]),
                          [f"scs{hh}", f"vv{hh}"], [f"idx{hh}"])
                kb.op("dve", lambda e: e.tensor_copy(out=idxf, in_=idx), [f"idx{hh}" for hh in range(16)], ["idxf"])
                kb.op("dve", lambda e: e.tensor_tensor(out=c4, in0=v4[:, :, 0, :].unsqueeze(3).broadcast_to([128, 8, 16, 16]),
                                                       in1=v4[:, :, 1, :].unsqueeze(2).broadcast_to([128, 8, 16, 16]),
                                                       op=ALU.add), [f"vv{hh}" for hh in range(16)], ["cand"])
                for h in range(8):
                    kb.op("dve", lambda e: e.max(out=ts[:, h, 0:8], in_=cand[:, h, :]), ["cand"], [f"ts{h}"])
                for h in range(8):
                    kb.op("dve", lambda e: e.max_index(out=pos[:, h, 0:8], in_max=ts[:, h, 0:8], in_values=cand[:, h, :]),
                          ["cand", f"ts{h}"], [f"pos{h}"])
                for h in range(8):
                    kb.op("dve", lambda e: e.match_replace(out=scr2[:, h, :], in_to_replace=ts[:, h, 0:8],
                                                           in_values=cand[:, h, :], imm_value=-1e30),
                          ["cand", f"ts{h}"], [f"scr2{h}"])
                for h in range(8):
                    kb.op("dve", lambda e: e.max(out=ts[:, h, 8:16], in_=scr2[:, h, :]), [f"scr2{h}"], [f"ts{h}"])
                for h in range(8):
                    kb.op("dve", lambda e: e.max_index(out=pos[:, h, 8:16], in_max=ts[:, h, 8:16], in_values=scr2[:, h, :]),
                          [f"scr2{h}", f"ts{h}"], [f"pos{h}"])
                tsk = [f"ts{h}" for h in range(8)]
                posk = [f"pos{h}" for h in range(8)]
                return

            def prepB(tt):
                tsk = [f"ts{h}" for h in range(8)]
                posk = [f"pos{h}" for h in range(8)]
                gate = sel3[:, 2, :].rearrange("p (h k) -> p h k", k=16)
                kb.op("dve", lambda e: e.tensor_tensor(out=gate, in0=ts, in1=ts[:, :, 0:1].broadcast_to([128, 8, 16]),
                                                       op=ALU.subtract), tsk, ["gate"])
                kb.op("act", lambda e: e.activation(out=gate, in_=gate, func=AF.Exp), ["gate"], ["gate"])
                kb.op("dve", lambda e: e.tensor_reduce(out=zs, in_=gate, axis=AX.X, op=ALU.add), ["gate"], ["zs"])
                kb.op("dve", lambda e: e.reciprocal(out=zs, in_=zs), ["zs"], ["zs"])
                kb.op("dve", lambda e: e.tensor_tensor(out=gate, in0=gate, in1=zs.unsqueeze(2).broadcast_to([128, 8, 16]),
                                                       op=ALU.mult), ["gate", "zs"], ["gate"])
                pu = pos.rearrange("p h k -> p (h k)")
                kb.op("dve", lambda e: e.tensor_single_scalar(out=apu, in_=pu, scalar=4, op=ALU.logical_shift_right),
                      posk, ["apu"])
                kb.op("dve", lambda e: e.tensor_single_scalar(out=bpu, in_=pu, scalar=15, op=ALU.bitwise_and),
                      posk, ["bpu"])
                kb.op("dve", lambda e: e.tensor_copy(out=aposf, in_=apu), ["apu"], ["aposf"])
                kb.op("dve", lambda e: e.tensor_copy(out=bposf, in_=bpu), ["bpu"], ["bposf"])
                for m, pp in ((0, aposf), (1, bposf)):
                    kb.op("dve", lambda e: e.tensor_tensor(out=eq, in0=pp.unsqueeze(2).broadcast_to([128, 128, 16]),
                                                           in1=colf[:, 0:16].unsqueeze(1).broadcast_to([128, 128, 16]),
                                                           op=ALU.is_equal), ["aposf", "bposf", "colf"], ["eq"])
                    kb.op("dve", lambda e: e.tensor_tensor(out=eq4, in0=eq4,
                                                           in1=i4[:, :, m, :].unsqueeze(2).broadcast_to([128, 8, 16, 16]),
                                                           op=ALU.mult), ["eq", "idxf"], ["eq"])
                    kb.op("dve", lambda e: e.tensor_reduce(out=sel3[:, m, :], in_=eq, axis=AX.X, op=ALU.add),
                          ["eq"], [f"sel{m}"])
                kb.op("dve", lambda e: e.tensor_copy(out=iu, in_=sel3[:, 0:2, :]), ["sel0", "sel1"], ["iu"])
                kb.op("dve", lambda e: e.tensor_scalar(out=g2, in0=sel3[:, 2, :], scalar1=2.0, scalar2=None, op0=ALU.mult),
                      ["gate"], ["g2"])
                kb.op("dve", lambda e: e.tensor_scalar(out=sbV[:, 7, :], in0=sel3[:, 2, :], scalar1=-6.0, scalar2=None,
                                                       op0=ALU.mult), ["gate"], ["sbV"])
                for bbit in range(7):
                    for m in range(2):
                        btt = bt[(bbit * 2 + m) % 4]
                        bkey = f"bt{(bbit * 2 + m) % 4}"
                        kb.op("dve", lambda e: e.tensor_scalar(out=btt, in0=iu[:, m, :], scalar1=bbit, scalar2=1,
                                                               op0=ALU.logical_shift_right, op1=ALU.bitwise_and),
                              ["iu"], [bkey])
                        if m == 0:
                            kb.op("dve", lambda e: e.tensor_scalar(out=sbU[:, bbit, :], in0=btt, scalar1=2.0, scalar2=-1.0,
                                                                   op0=ALU.mult, op1=ALU.add), [bkey], ["sbU"])
                        else:
                            kb.op("dve", lambda e: e.scalar_tensor_tensor(out=sbV[:, bbit, :], in0=btt, scalar=0.5, in1=g2,
                                                                          op0=ALU.subtract, op1=ALU.mult),
                                  [bkey, "g2"], ["sbV"])
                pbase = (tt % 2) * 32
                kb.dma(bitsd[tt, 0], sbU, ["sbU", "sbU7"], [f"bitsd{tt}u"], "st_bitsU")
                kb.dma(bitsd[tt, 1], sbV, ["sbV"], [f"bitsd{tt}v"], "st_bitsV")
                kb.dma(BUV[pbase:pbase + 8, :].rearrange("b (t j) -> b t j", j=128), bitsd[tt, 0].rearrange("t b j -> b t j"),
                       [f"bitsd{tt}u"], [f"BUVu{tt % 2}"], f"ld_bitsU{tt % 2}")
                kb.dma(BUV[pbase + 8:pbase + 16, :].rearrange("b (t j) -> b t j", j=128),
                       bitsd[tt, 1].rearrange("t b j -> b t j"), [f"bitsd{tt}v"], [f"BUVv{tt % 2}"], f"ld_bitsV{tt % 2}")
            def gloop(tt, mid=None):
                pbase = (tt % 2) * 32
                gs = Gs[0]
                bkeys = [f"BUVu{tt % 2}", f"BUVv{tt % 2}"]

                def uv(g):
                    par_ = g % 2
                    for tq in range(8):
                        t = g * 8 + tq
                        kb.op("pe", lambda e: e.matmul(out=qd[par_][:, tq * 128:(tq + 1) * 128],
                                                       lhsT=BUV[pbase:pbase + 16, t * 128:(t + 1) * 128], rhs=sgnU[pbase:pbase + 16, :],
                                                       start=True, stop=True), bkeys + ["sgnU"], BK(4 * par_ + tq // 4))
                    for tq in range(8):
                        t = g * 8 + tq
                        kb.op("pe", lambda e: e.matmul(out=qd[par_][:, 1024 + tq * 128:1024 + (tq + 1) * 128],
                                                       lhsT=BUV[pbase:pbase + 16, t * 128:(t + 1) * 128],
                                                       rhs=sgnV[pbase:pbase + 16, :], start=True, stop=True),
                              bkeys + ["sgnV"], BK(4 * par_ + 2 + tq // 4))
                    sl = g % NUB
                    kb.op("act", lambda e: e.activation(out=Ub[sl].rearrange("p t i -> p (t i)"), in_=qd[par_][:, 0:1024],
                                                        func=AF.Relu), BK(4 * par_) + BK(4 * par_ + 1), [f"Ub{sl}"])
                    kb.op("act", lambda e: e.activation(out=Vb[sl].rearrange("p t i -> p (t i)"), in_=qd[par_][:, 1024:2048],
                                                        func=AF.Relu), BK(4 * par_ + 2) + BK(4 * par_ + 3), [f"Vb{sl}"])

                def gmm(g):
                    par_ = g % 2
                    sl = g % NUB
                    for tq in range(8):
                        kb.op("pe", lambda e: e.matmul(out=qd[par_][:, tq * 128:(tq + 1) * 128],
                                                       lhsT=Ub[sl][:, tq, :], rhs=Vb[sl][:, tq, :], start=True, stop=True),
                              [f"Ub{sl}", f"Vb{sl}"], BK(4 * par_ + tq // 4))
                    kb.op("act", lambda e: e.activation(out=gs[:, :, g * 8:(g + 1) * 8],
                                                        in_=qd[par_][:, 0:1024].rearrange("p (t b) -> p b t", b=128),
                                                        func=AF.Copy), BK(4 * par_) + BK(4 * par_ + 1), ["Gs0"])

                uv(0)
                for g in range(16):
                    if g == 8 and mid is not None:
                        mid()
                    if g + 1 < 16:
                        uv(g + 1)
                    gmm(g)
                kb.dma(Gd[tt].rearrange("g p f -> p g f"), gs.rearrange("p (g b) t -> p g (b t)", b=8),
                       ["Gs0"], [f"Gd{tt}"], "st_G0")

            burst(0)
            prepA(0)
            prepB(0)
            for tt in range(16):
                if tt + 1 < 16:
                    burst(tt + 1)
                    prepA(tt + 1)
                    gloop(tt, mid=lambda: prepB(tt + 1))
                else:
                    gloop(tt)
            for n in (["wq", "keysTb", "Gs0", "qT", "vv", "idx", "idxf", "cand", "scr2", "ts", "pos", "apu", "bpu",
                       "aposf", "bposf", "zs", "eq", "sel3", "scs", "BUV", "sgnU", "sgnV", "sgi", "sgf", "pm", "pmf", "sbU", "sbV", "iu", "g2",
                       "bt0", "bt1", "bt2", "bt3"] + [f"Ub{i}" for i in range(NUB)] + [f"Vb{i}" for i in range(NUB)]):
                del sb.blocks[n]
            kb.barrier()

        run_e = upto >= 7
        if run_e:
            if "stg0" not in sb.blocks:
                stg[0] = sb.alloc("stg0", [128, 2048], F32)
                stg[1] = sb.alloc("stg1", [128, 2048], F32)
            yacc = sb.alloc("yacc", [128, 16, D], F32)
            wdb = [sb.alloc(f"wdb{i}", [128, 8, 1024], BF16) for i in range(2)]
            wub = [sb.alloc(f"wub{i}", [128, 8, 1024], BF16) for i in range(2)]
            Gt = [sb.alloc(f"Gt{i}", [128, 2, 1024], BF16) for i in range(2)]
            NG = 5
            gel = [sb.alloc(f"gel{i}", [128, 256], BF16) for i in range(NG)]
            At = [sb.alloc(f"At{i}", [128, 256], BF16) for i in range(NG)]
            wdT_v = wdT.rearrange("(k p) c -> p k c", p=128)
            w_up_v = w_up.rearrange("(i b) d -> i b d", b=128)

            def load_w_gen(g):
                ws = g % 2
                eng = "dve" if g == 0 else "pool"
                yield from cast_rows_gen(wdb[ws], wdT_v[:, :, g * 1024:(g + 1) * 1024], P("nfw"), f"wdb{ws}", 8, 1024, eng)
                yield from cast_rows_gen(wub[ws], w_up_v[:, g * 8:(g + 1) * 8, :], None, f"wub{ws}", 8, 1024, eng)

            def load_G(g, T):
                gsl = (g * 8 + T) % 2
                kb.dma(Gt[gsl], Gd[2 * T:2 * T + 2, g, :, :].rearrange("s p f -> p s f"),
                       [f"Gd{2 * T}", f"Gd{2 * T + 1}"], [f"Gt{gsl}"], f"ld_Gt{gsl}")

            NGRP = 16
            for _ in load_w_gen(0):
                pass
            load_G(0, 0)
            kb.dma(yacc, h2d.rearrange("(r p) d -> p r d", p=128), [f"h2d{r}" for r in range(16)], ["yacc"], "ld_yacc")
            si = 0
            DEPTH = 3
            for g in range(NGRP):
                ws = g % 2
                wgen = load_w_gen(g + 1) if g + 1 < NGRP else iter(())
                for T in range(8):
                    gsl = (g * 8 + T) % 2
                    if T + 1 < 8:
                        load_G(g, T + 1)
                    elif g + 1 < NGRP:
                        load_G(g + 1, 0)
                    for _ in range(3 if T < 7 else 99):
                        if next(wgen, "done") == "done":
                            break
                    tk = slice(T * 256, (T + 1) * 256)
                    pend = []

                    def emit_y(bb, asl):
                        for sub in range(2):
                            for dh in range(2):
                                yb_ = sub * 2 + dh
                                kb.op("pe", lambda e: e.matmul(out=bk[yb_][:, :], lhsT=At[asl][:, sub * 128:(sub + 1) * 128],
                                                               rhs=wub[ws][:, bb, dh * 512:(dh + 1) * 512],
                                                               start=(bb == 0), stop=(bb == 7)),
                                      [f"At{asl}", f"wub{ws}"], BK(yb_))

                    for bb in range(8):
                        sbk = 4 + si % 4
                        asl = si % NG
                        si += 1
                        for dc in range(8):
                            kb.op("pe", lambda e: e.matmul(out=bk[sbk][:, 0:256], lhsT=wdb[ws][:, dc, bb * 128:(bb + 1) * 128],
                                                           rhs=xnT[:, dc, tk], start=(dc == 0), stop=(dc == 7)),
                                  [f"wdb{ws}", f"xnT{2 * T}", f"xnT{2 * T + 1}"], BK(sbk))
                        kb.op("act", lambda e: e.activation(out=gel[asl], in_=bk[sbk][:, 0:256], func=AF.Gelu),
                              BK(sbk), [f"gel{asl}"])
                        kb.op("dve", lambda e: e.tensor_tensor(out=At[asl].rearrange("p (s t) -> p s t", t=128),
                                                               in0=gel[asl].rearrange("p (s t) -> p s t", t=128),
                                                               in1=Gt[gsl][:, :, bb * 128:(bb + 1) * 128], op=ALU.mult),
                              [f"gel{asl}", f"Gt{gsl}"], [f"At{asl}"])
                        pend.append((bb, asl))
                        if len(pend) > DEPTH:
                            emit_y(*pend.pop(0))
                    while pend:
                        emit_y(*pend.pop(0))
                    for sub in range(2):
                        for dh in range(2):
                            yb_ = sub * 2 + dh
                            ya = yacc[:, 2 * T + sub, dh * 512:(dh + 1) * 512]
                            kb.op("dve", lambda e: e.tensor_tensor(out=ya, in0=bk[yb_][:, :], in1=ya, op=ALU.add),
                                  BK(yb_) + [f"yacc{2 * T + sub}", "yacc"], [f"yacc{2 * T + sub}"])

            nfb = sb.alloc("nfb", [128, D], F32)
            kb.dma(nfb, nfinal.partition_broadcast(128), [], ["nfb"], "ld_nfb")
            ssF = sb.alloc("ssF", [128, 16], F32)
            junkF = gel[0]
            ot = [wdb[0].rearrange("p k c -> p (k c)").bitcast(F32)[:, 0:D], wdb[1].rearrange("p k c -> p (k c)").bitcast(F32)[:, 0:D]]
            jf = wub[0].rearrange("p k c -> p (k c)").bitcast(F32)[:, 0:D]
            for r in range(16):
                s = r % 2
                kb.op("act", lambda e: e.activation(out=jf, in_=yacc[:, r, :], func=AF.Square, accum_out=ssF[:, r:r + 1]),
                      [f"yacc{r}", "yacc"], ["wub0", f"ssF{r}"])
                rstd_from_ss(ssF[:, r:r + 1], ssF[:, r:r + 1], D, [f"ssF{r}"], [f"ssF{r}"])
                kb.op("dve", lambda e: e.scalar_tensor_tensor(out=ot[s], in0=yacc[:, r, :], scalar=ssF[:, r:r + 1], in1=nfb,
                                                              op0=ALU.mult, op1=ALU.mult),
                      [f"yacc{r}", "yacc", f"ssF{r}", "nfb"], [f"wdb{s}"])
                kb.dma(out[r * 128:(r + 1) * 128, :], ot[s], [f"wdb{s}"], [f"out{r}"], f"st_out{s}")
            out_keys = [f"out{r}" for r in range(16)]
        else:
            out_keys = []

        fin = list(out_keys)
        if "h2" in dbg:
            fin += [f"h2d{r}" for r in range(16)]
        if "G" in dbg:
            fin += [f"Gd{r}" for r in range(16)]
        if "ycT" in dbg:
            d_ = dbg_tensor("ycT", [128, 8, SEQ], BF16)
            kb.dma(d_, ycT, [f"ycT{tb}" for tb in range(8)], ["dbg_ycT"], "st_dbg")
            fin.append("dbg_ycT")
        if "ynT" in dbg:
            d_ = dbg_tensor("ynT", [128, 8, SEQ], BF16)
            kb.dma(d_, ynT[:, :, 0:SEQ], ynT_keys, ["dbg_ynT"], "st_dbg")
            fin.append("dbg_ynT")
        if "uT" in dbg:
            d_ = dbg_tensor("uT", [128, 8, TP], BF16)
            kb.dma(d_, uT, uT_keys, ["dbg_uT"], "st_dbg")
            fin.append("dbg_uT")
        kb.wait_all("sp", fin)
        print("instructions:", kb.n_ins)
    return nc, list(dbg_out.keys())


def make_inputs(inp, b):
    f = lambda a: np.ascontiguousarray(np.asarray(a, dtype=np.float32))
    x = f(inp["x"])[b]
    meta = f(inp["meta_tokens"])
    xpad = np.concatenate([np.zeros((112, D), np.float32), meta, x], axis=0)
    return xpad


def shared_inputs(inp):
    f = lambda a: np.ascontiguousarray(np.asarray(a, dtype=np.float32))
    pv = lambda v, n: f(v).reshape(n, 128).T
    par = np.zeros((128, NPAR), np.float32)

    def put(name, arr):
        o, w = PO[name]
        assert arr.shape == (128, w), (name, arr.shape)
        par[:, o:o + w] = arr

    put("nmw", pv(inp["norm_mix_w"][0], 8))
    put("nfw", pv(inp["norm_ffn_w"][0], 8))
    put("snw", pv(inp["ssd_norm_w"][0], 8))
    scw = f(inp["ssd_conv_w"][0])
    put("scw", scw.T.reshape(16, 128, 4).transpose(1, 0, 2).reshape(128, 64))
    put("scb", pv(inp["ssd_conv_b"][0], 16))
    ccw = f(inp["conf_conv_w"][0])
    put("ccw", ccw.T.reshape(8, 128, 31).transpose(1, 0, 2).reshape(128, 248))
    put("ccb", pv(inp["conf_conv_b"][0], 8))
    put("clg", pv(inp["conf_ln_g"][0], 8))
    put("clb", pv(inp["conf_ln_b"][0], 8))
    put("Drep", np.broadcast_to(f(inp["ssd_D"][0])[None, :], (128, 16)))
    put("dtb", np.broadcast_to(f(inp["ssd_dt_bias"][0])[None, :], (128, 16)))
    put("alog", np.broadcast_to(f(inp["ssd_A_log"][0])[None, :], (128, 16)))
    m = np.ones((128, 1), np.float32)
    m[:112] = 0.0
    put("dtmask", m)
    k1 = f(inp["peer_sub_keys_1"][0])
    k2 = f(inp["peer_sub_keys_2"][0])
    keys = np.stack([k1, k2], axis=1).reshape(16, 128, 128)
    keysT = np.ascontiguousarray(keys.transpose(2, 0, 1))
    wd = f(inp["peer_w_down"][0]).reshape(128, 128, D)
    wdT = np.ascontiguousarray(wd.transpose(2, 1, 0)).reshape(D, 16384)
    return {
        "w_in": f(inp["w_in"][0]), "w_out": f(inp["w_out"][0]), "w_q": f(inp["peer_w_query"][0]),
        "keysT": keysT, "wdT": wdT, "w_up": f(inp["peer_w_up"][0]), "params": par,
        "nfinal": f(inp["norm_final_w"]),
    }


_CACHE = {}


def kernel(**inputs):
    if "nc" not in _CACHE:
        _CACHE["nc"] = build_program()[0]
    nc = _CACHE["nc"]
    sh = shared_inputs(inputs)
    in_maps = []
    for b in range(8):
        m = dict(sh)
        m["xpad"] = make_inputs(inputs, b)
        in_maps.append(m)
    res = run_bass_kernel_spmd(nc, in_maps, core_ids=list(range(8)))
    return np.stack([np.asarray(r["out"], dtype=np.float32) for r in res.results], axis=0)
```

```python
import contextlib
import numpy as np
import concourse.bass as bass
import concourse.mybir as mybir
from concourse.bass_utils import run_bass_kernel_spmd

F32 = mybir.dt.float32
BF16 = mybir.dt.bfloat16
U32 = mybir.dt.uint32
I32 = mybir.dt.int32
AF = mybir.ActivationFunctionType
ALU = mybir.AluOpType
AX = mybir.AxisListType

D = 1024
SEQ = 2048
NTILE = 17
TP = NTILE * 128
D_IN = 5136
EPS = 1e-5
SAME_ENGINE_SYNC = True
SSD_STEPS = 99
S2CUT = 99
S2SKIP = ()

PO = {}
_o = 0
for _n, _w in [("nmw", 8), ("nfw", 8), ("snw", 8), ("scw", 64), ("scb", 16), ("ccw", 248), ("ccb", 8),
               ("clg", 8), ("clb", 8), ("Drep", 16), ("dtb", 16), ("alog", 16), ("dtmask", 1)]:
    PO[_n] = (_o, _w)
    _o += _w
NPAR = _o


class KB:
    def __init__(self, nc, es):
        self.nc = nc
        self.es = es
        self.eng = {"pe": nc.tensor, "act": nc.scalar, "dve": nc.vector, "pool": nc.gpsimd, "sp": nc.sync}
        self.sems = {}
        self.cnt = {}
        self.waited = {e: {} for e in self.eng}
        self.res = {}
        for e in ("pe", "act", "dve", "pool"):
            self.new_sem(e)
        self.n_ins = 0

    def new_sem(self, key):
        self.sems[key] = self.es.enter_context(self.nc.semaphore("s_" + key))
        self.cnt[key] = 0

    def _deps(self, reads, writes):
        d = {}

        def add(tok):
            k, v = tok
            if d.get(k, 0) < v:
                d[k] = v

        for k in reads:
            w = self.res.get(k)
            if w and w[0]:
                add(w[0])
        for k in writes:
            w = self.res.get(k)
            if w:
                if w[0]:
                    add(w[0])
                for tok in w[1].items():
                    add(tok)
        return d

    def _commit(self, reads, writes, tok):
        for k in reads:
            w = self.res.setdefault(k, [None, {}])
            if w[1].get(tok[0], 0) < tok[1]:
                w[1][tok[0]] = tok[1]
        for k in writes:
            self.res[k] = [tok, {}]

    @staticmethod
    def _norm(reads, writes):
        r2, w2 = [], []
        for k in reads:
            if k.startswith("bank"):
                k = k.split("q")[0]
                if k not in w2:
                    w2.append(k)
            else:
                r2.append(k)
        for k in writes:
            if k.startswith("bank"):
                k = k.split("q")[0]
            if k not in w2:
                w2.append(k)
        return r2, w2

    def _need(self, e, reads, writes):
        d = self._deps(reads, writes)
        need = []
        for k, v in d.items():
            if self.waited[e].get(k, 0) >= v:
                continue
            if k == e and (e == "pe" or not SAME_ENGINE_SYNC):
                continue
            need.append((k, v))
        return need

    def op(self, e, fn, reads=(), writes=()):
        reads, writes = self._norm(reads, writes)
        need = self._need(e, reads, writes)
        for k, v in need[:-1]:
            self.eng[e].wait_ge(self.sems[k], v)
            self.waited[e][k] = v
        ins = fn(self.eng[e])
        if need:
            k, v = need[-1]
            ins._wait_ge(self.sems[k], v)
            self.waited[e][k] = v
        self.cnt[e] += 1
        ins.then_inc(self.sems[e], 1)
        self._commit(reads, writes, (e, self.cnt[e]))
        self.n_ins += 1
        return ins

    def dma(self, out, in_, reads, writes, sem, q="sp"):
        if sem not in self.sems:
            self.new_sem(sem)
        reads, writes = self._norm(reads, writes)
        need = self._need(q, reads, writes)
        for k, v in need:
            self.eng[q].wait_ge(self.sems[k], v)
            self.waited[q][k] = v
        ins = self.eng[q].dma_start(out=out, in_=in_)
        self.cnt[sem] += 16
        ins.then_inc(self.sems[sem], 16)
        self._commit(reads, writes, (sem, self.cnt[sem]))
        self.n_ins += 1
        return ins

    def barrier(self):
        for e in self.eng:
            for k, v in self.cnt.items():
                if v > 0 and self.waited[e].get(k, 0) < v and k != e:
                    self.eng[e].wait_ge(self.sems[k], v)
                    self.waited[e][k] = v

    def wait_all(self, e, keys):
        need = self._need(e, keys, ())
        for k, v in need:
            self.eng[e].wait_ge(self.sems[k], v)
            self.waited[e][k] = v


class SB:
    def __init__(self, nc, base, cap, kb):
        self.nc = nc
        self.kb = kb
        self.base = base
        self.cap = cap
        self.blocks = {}
        self.uid = 0

    def alloc(self, name, shape, dtype):
        esz = {F32: 4, BF16: 2, U32: 4, I32: 4}[dtype]
        n = 1
        for s in shape[1:]:
            n *= s
        size = ((n * esz + 63) // 64) * 64
        used = sorted(self.blocks.values())
        off = self.base
        for (o, s) in used:
            if off + size <= o:
                break
            off = max(off, o + s)
        assert off + size <= self.cap, f"SBUF overflow allocating {name} ({size}B at {off}, cap {self.cap})"
        self.blocks[name] = (off, size)
        self.uid += 1
        t = self.nc.alloc_sbuf_tensor_at(f"{name}_{self.uid}", list(shape), dtype, offset=off)
        return t.ap()

    def free(self, *names):
        for n in names:
            del self.blocks[n]
        self.kb.barrier()


def build_program(dbg=(), upto=99, ssd_chunks=NTILE):
    nc = bass.Bass("TRN2", target_bir_lowering=False)
    dram = {}

    def din(name, shape, dt=F32):
        dram[name] = nc.dram_tensor(name, list(shape), dt, kind="ExternalInput").ap()
        return dram[name]

    xpad = din("xpad", [TP, D])
    w_in = din("w_in", [D, D_IN])
    w_out = din("w_out", [2 * D, D])
    w_q = din("w_q", [D, 2048])
    keysT = din("keysT", [128, 16, 128])
    wdT = din("wdT", [D, 16384])
    w_up = din("w_up", [16384, D])
    params = din("params", [128, NPAR])
    nfinal = din("nfinal", [D])
    out = nc.dram_tensor("out", [SEQ, D], F32, kind="ExternalOutput").ap()
    h2d = nc.dram_tensor("h2d", [SEQ, D], F32, kind=("ExternalOutput" if "h2" in dbg else "Internal")).ap()
    Gd = nc.dram_tensor("Gd", [16, 16, 128, 1024], BF16, kind=("ExternalOutput" if "G" in dbg else "Internal")).ap()
    bitsd = nc.dram_tensor("bitsd", [16, 2, 128, 8, 128], BF16, kind="Internal").ap()
    dbg_out = {}

    def dbg_tensor(name, shape, dt=F32):
        dbg_out[name] = nc.dram_tensor("dbg_" + name, list(shape), dt, kind="ExternalOutput").ap()
        return dbg_out[name]

    es = contextlib.ExitStack()
    with es:
        kb = KB(nc, es)
        sb = SB(nc, ((nc.sbuf_base + 63) // 64) * 64, (nc.sbuf_top // 64) * 64, kb)
        quads = [es.enter_context(nc.psum_tensor(f"quad{i}", [128, 2048], F32)) for i in range(2)]
        qd = [q_[:, :] for q_ in quads]
        bk = [qd[i // 4][:, (i % 4) * 512:(i % 4 + 1) * 512] for i in range(8)]

        def BK(i):
            return [f"bank{i}q{q}" for q in range(4)]

        def BKH(i, h):
            return [f"bank{i}q{2 * h}", f"bank{i}q{2 * h + 1}"]

        def BKQ(i, q):
            return [f"bank{i}q{q}"]

        def bank_bf(i):
            return bk[i].bitcast(BF16)

        par = sb.alloc("par", [128, NPAR], F32)
        kb.dma(par, params, [], ["par"], "ld_par")

        def P(name, a=None, b=None):
            o, w = PO[name]
            if a is None:
                return par[:, o:o + w]
            return par[:, o + a:o + (b if b is not None else a + 1)]

        coli = sb.alloc("coli", [128, 128], I32)
        pidi = sb.alloc("pidi", [128, 1], I32)
        colf = sb.alloc("colf", [128, 128], F32)
        pidf = sb.alloc("pidf", [128, 1], F32)
        identf = sb.alloc("identf", [128, 128], F32)
        identb = sb.alloc("identb", [128, 128], BF16)
        triU = sb.alloc("triU", [128, 128], F32)
        maskneg = sb.alloc("maskneg", [128, 128], BF16)
        onesf = sb.alloc("onesf", [128, 128], F32)
        iotab = sb.alloc("iotab", [128, 128], BF16)
        kb.op("pool", lambda e: e.iota(coli, pattern=[[1, 128]], base=0, channel_multiplier=0), [], ["coli"])
        kb.op("pool", lambda e: e.iota(pidi, pattern=[[0, 1]], base=0, channel_multiplier=1), [], ["pidi"])
        kb.op("dve", lambda e: e.tensor_copy(out=colf, in_=coli), ["coli"], ["colf"])
        kb.op("dve", lambda e: e.tensor_copy(out=pidf, in_=pidi), ["pidi"], ["pidf"])
        kb.op("dve", lambda e: e.tensor_scalar(out=identf, in0=colf, scalar1=pidf[:, 0:1], scalar2=None,
                                               op0=ALU.is_equal), ["colf", "pidf"], ["identf"])
        kb.op("dve", lambda e: e.tensor_copy(out=identb, in_=identf), ["identf"], ["identb"])
        kb.op("dve", lambda e: e.tensor_scalar(out=triU, in0=colf, scalar1=pidf[:, 0:1], scalar2=None,
                                               op0=ALU.is_ge), ["colf", "pidf"], ["triU"])
        kb.op("dve", lambda e: e.tensor_scalar(out=maskneg, in0=triU, scalar1=-1.0, scalar2=30000.0,
                                               op0=ALU.add, op1=ALU.mult), ["triU"], ["maskneg"])
        kb.op("dve", lambda e: e.memset(onesf, 1.0), [], ["onesf"])
        kb.op("dve", lambda e: e.tensor_copy(out=iotab, in_=colf), ["colf"], ["iotab"])

        cdiag = sb.alloc("cdiag", [128, 8, 31, 128], BF16)
        def build_cdiag(cc):
            for k in range(31):
                kb.op("pool", lambda e: e.tensor_scalar(out=cdiag[:, cc, k, :], in0=identf,
                                                        scalar1=P("ccw", cc * 31 + k), scalar2=1.0,
                                                        op0=ALU.mult, op1=ALU.mult),
                      ["identf", "par"], [f"cdiag{cc}"])
        stg = [sb.alloc(f"stg{i}", [128, 2048], F32) for i in range(2)]
        stg_i = [0]

        def cast_rows(dst, src, scale, key, kcn, n, eng="pool"):
            for _ in cast_rows_gen(dst, src, scale, key, kcn, n, eng):
                pass

        def cast_rows_gen(dst, src, scale, key, kcn, n, eng="pool"):
            per = max(1, 2048 // n)
            for k0 in range(0, kcn, per):
                k1 = min(kcn, k0 + per)
                s = stg_i[0] % 2
                stg_i[0] += 1
                sv = stg[s][:, 0:(k1 - k0) * n].rearrange("p (k n) -> p k n", n=n)
                kb.dma(sv, src[:, k0:k1, :], [], [f"stg{s}"], f"ld_stg{s}")
                if scale is None:
                    kb.op(eng, lambda e, sv=sv, k0=k0, k1=k1: e.tensor_copy(out=dst[:, k0:k1, :], in_=sv),
                          [f"stg{s}"], [key])
                else:
                    for k in range(k0, k1):
                        kb.op(eng, lambda e, sv=sv, k=k, k0=k0: e.tensor_scalar(
                            out=dst[:, k, :], in0=sv[:, k - k0, :], scalar1=scale[:, k:k + 1], scalar2=1.0,
                            op0=ALU.mult, op1=ALU.mult), [f"stg{s}", "par"], [key])
                yield

        w_in_v = w_in.rearrange("(k p) c -> p k c", p=128)

        def rstd_from_ss(dst, ss, n, keyr, keyw, lnexp=False):
            kb.op("dve", lambda e: e.tensor_scalar(out=dst, in0=ss, scalar1=1.0 / n, scalar2=EPS,
                                                   op0=ALU.mult, op1=ALU.add), keyr, keyw)
            if lnexp:
                kb.op("act", lambda e: e.activation(out=dst, in_=dst, func=AF.Ln), keyw, keyw)
                kb.op("act", lambda e: e.activation(out=dst, in_=dst, func=AF.Exp, scale=-0.5), keyw, keyw)
            else:
                kb.op("act", lambda e: e.activation(out=dst, in_=dst, func=AF.Sqrt), keyw, keyw)
                kb.op("dve", lambda e: e.reciprocal(out=dst, in_=dst), keyw, keyw)

        uT = sb.alloc("uT", [128, 8, TP], BF16)
        ssA = sb.alloc("ssA", [128, NTILE], F32)
        xin = [sb.alloc(f"xin{i}", [128, D], F32) for i in range(4)]
        ub = [sb.alloc(f"ub{i}", [128, D], BF16) for i in range(3)]
        junk = sb.alloc("junk", [128, D], F32)
        def a_stageA(t):
            sx = t % 4
            kb.dma(xin[sx], xpad[t * 128:(t + 1) * 128, :], [], [f"xin{sx}"], f"ld_xin{sx}")
            kb.op("act", lambda e: e.activation(out=junk, in_=xin[sx], func=AF.Square, accum_out=ssA[:, t:t + 1]),
                  [f"xin{sx}"], ["junk", f"ssA{t}"])

        def a_stageB(t):
            sx, s = t % 4, t % 3
            rstd_from_ss(ssA[:, t:t + 1], ssA[:, t:t + 1], D, [f"ssA{t}"], [f"ssA{t}"])
            kb.op("dve", lambda e: e.tensor_scalar(out=ub[s], in0=xin[sx], scalar1=ssA[:, t:t + 1], scalar2=None,
                                                   op0=ALU.mult), [f"xin{sx}", f"ssA{t}"], [f"ub{s}"])

        def a_stageC(t):
            s = t % 3
            pb = bank_bf(t % 2)
            for dc in range(8):
                kb.op("pe", lambda e: e.transpose(out=pb[:, dc * 128:(dc + 1) * 128],
                                                  in_=ub[s][:, dc * 128:(dc + 1) * 128], identity=identb),
                      [f"ub{s}", "identb"], BK(t % 2))
            kb.op("act", lambda e: e.activation(out=uT[:, :, t * 128:(t + 1) * 128],
                                                in_=pb.rearrange("p (k n) -> p k n", n=128), func=AF.Copy),
                  BK(t % 2), [f"uT{t}"])

        for step in range(NTILE + 2):
            if step < NTILE:
                a_stageA(step)
            if 1 <= step <= NTILE:
                a_stageB(step - 1)
            if step >= 2:
                a_stageC(step - 2)
        uT_keys = [f"uT{t}" for t in range(NTILE)]
        sb.free("xin0", "xin1", "xin2", "xin3", "ub0", "ub1", "ub2", "junk", "ssA")

        GW = 30 + TP
        glu = sb.alloc("glu", [128, 8, GW], BF16)
        kb.op("pool", lambda e: e.memset(glu[:, :, 0:30], 0.0), [], ["glupad"])
        wag = [sb.alloc(f"wag{i}", [128, 8, 256], BF16) for i in range(2)]
        sig = [sb.alloc(f"sig{i}", [128, 512], F32) for i in range(2)]
        CA0 = 3088
        blocks = [(i * 512, 512) for i in range(4)] + [(2048, 128)]
        bi = 0
        for cc in range(8):
            ws = cc % 2
            cast_rows(wag[ws][:, :, 0:128], w_in_v[:, :, CA0 + cc * 128:CA0 + (cc + 1) * 128], P("nmw"), f"wag{ws}", 8, 128)
            cast_rows(wag[ws][:, :, 128:256], w_in_v[:, :, CA0 + 1024 + cc * 128:CA0 + 1024 + (cc + 1) * 128], P("nmw"),
                      f"wag{ws}", 8, 128)
            if cc >= 1:
                build_cdiag(cc - 1)
            for (t0, n) in blocks:
                ba, bg = (2 * bi) % 8, (2 * bi + 1) % 8
                bi += 1
                tkeys = [f"uT{t}" for t in range(t0 // 128, (t0 + n) // 128)]
                for (bb, c0) in ((ba, 0), (bg, 128)):
                    for dc in range(8):
                        kb.op("pe", lambda e: e.matmul(out=bk[bb][:, 0:n], lhsT=wag[ws][:, dc, c0:c0 + 128],
                                                       rhs=uT[:, dc, t0:t0 + n], start=(dc == 0), stop=(dc == 7)),
                              [f"wag{ws}"] + tkeys, BK(bb))
                sg = sig[bi % 2]
                kb.op("act", lambda e: e.activation(out=sg[:, 0:n], in_=bk[bg][:, 0:n], func=AF.Sigmoid),
                      BK(bg), [f"sig{bi % 2}"])
                kb.op("dve", lambda e: e.tensor_tensor(out=glu[:, cc, 30 + t0:30 + t0 + n], in0=bk[ba][:, 0:n],
                                                       in1=sg[:, 0:n], op=ALU.mult),
                      BK(ba) + [f"sig{bi % 2}"], [f"glu{cc}"])
        build_cdiag(7)
        sb.free("wag0", "wag1", "sig0", "sig1")

        ycT = sb.alloc("ycT", [128, 8, SEQ], BF16)
        sb.free("stg0", "stg1")
        hbs = [sb.alloc(f"hb{i}", [128, 8, 256], F32) for i in range(2)]
        hsq = sb.alloc("hsq", [128, 2, 256], F32)
        mean = sb.alloc("mean", [128, 256], F32)
        var = sb.alloc("var", [128, 256], F32)
        cvi = [0]

        def conf_conv(tb):
            hb = hbs[tb % 2]
            hbn = f"hb{tb % 2}_"
            s1b, s2b = (6, 7) if tb % 2 == 0 else (4, 5)
            P0 = 128 + tb * 256
            pend = None
            for cc in range(8):
                bb = cvi[0] % 4
                cvi[0] += 1
                for k in range(31):
                    kb.op("pe", lambda e: e.matmul(out=bk[bb][:, 0:256], lhsT=cdiag[:, cc, k, :],
                                                   rhs=glu[:, cc, P0 + k:P0 + k + 256], start=(k == 0), stop=(k == 30)),
                          [f"cdiag{cc}", f"glu{cc}", "glupad"], BK(bb))
                kb.op("act", lambda e: e.activation(out=hb[:, cc, :], in_=bk[bb][:, 0:256], func=AF.Identity,
                                                    bias=P("ccb", cc)), BK(bb) + ["par"], [hbn + str(cc)])
                kb.op("act", lambda e: e.activation(out=hsq[:, cc % 2, :], in_=bk[bb][:, 0:256], func=AF.Square,
                                                    bias=P("ccb", cc)), BK(bb) + ["par"], [f"hsq{cc % 2}"])
                if pend is not None:
                    pc = pend
                    kb.op("pe", lambda e: e.matmul(out=bk[s2b][:, 0:256], lhsT=onesf, rhs=hsq[:, pc % 2, :],
                                                   start=(pc == 0), stop=False), ["onesf", f"hsq{pc % 2}"], BK(s2b))
                pend = cc
            kb.op("pe", lambda e: e.matmul(out=bk[s2b][:, 0:256], lhsT=onesf, rhs=hsq[:, 1, :],
                                           start=False, stop=True), ["onesf", "hsq1"], BK(s2b))

        def conf_tail(tb):
            hb = hbs[tb % 2]
            hbn = f"hb{tb % 2}_"
            s1b, s2b = (6, 7) if tb % 2 == 0 else (4, 5)
            mean, var = mvs[tb % 2]
            mk, vk = f"mean{tb % 2}", f"var{tb % 2}"
            for cc in range(8):
                kb.op("pe", lambda e: e.matmul(out=bk[s1b][:, 0:256], lhsT=onesf, rhs=hb[:, cc, :],
                                               start=(cc == 0), stop=(cc == 7)), ["onesf", hbn + str(cc)], BK(s1b))
            kb.op("dve", lambda e: e.tensor_scalar(out=mean, in0=bk[s1b][:, 0:256], scalar1=1.0 / D, scalar2=None,
                                                   op0=ALU.mult), BK(s1b), [mk])
            kb.op("dve", lambda e: e.tensor_tensor(out=var, in0=mean, in1=mean, op=ALU.mult), [mk], [vk])
            kb.op("dve", lambda e: e.scalar_tensor_tensor(out=var, in0=bk[s2b][:, 0:256], scalar=1.0 / D, in1=var,
                                                          op0=ALU.mult, op1=ALU.subtract), BK(s2b) + [vk], [vk])
            kb.op("dve", lambda e: e.tensor_scalar(out=var, in0=var, scalar1=EPS, scalar2=None, op0=ALU.add), [vk], [vk])
            kb.op("act", lambda e: e.activation(out=var, in_=var, func=AF.Sqrt), [vk], [vk])
            kb.op("dve", lambda e: e.reciprocal(out=var, in_=var), [vk], [vk])
            hkeys = [hbn + str(cc) for cc in range(8)]
            kb.op("dve", lambda e: e.tensor_tensor(out=hb, in0=hb, in1=mean.unsqueeze(1).broadcast_to([128, 8, 256]),
                                                   op=ALU.subtract), hkeys + [mk], hkeys)
            kb.op("dve", lambda e: e.tensor_tensor(out=hb, in0=hb, in1=var.unsqueeze(1).broadcast_to([128, 8, 256]),
                                                   op=ALU.mult), hkeys + [vk], hkeys)
            for cc in range(8):
                kb.op("act", lambda e: e.activation(out=ycT[:, cc, tb * 256:(tb + 1) * 256], in_=hb[:, cc, :],
                                                    func=AF.Silu, scale=P("clg", cc), bias=P("clb", cc)),
                      [hbn + str(cc), "par"], [f"ycT{tb}"])

        mvs = [(mean, var), (sb.alloc("mean1", [128, 256], F32), sb.alloc("var1", [128, 256], F32))]
        conf_conv(0)
        for tb in range(1, 8):
            conf_conv(tb)
            conf_tail(tb - 1)
        conf_tail(7)
        sb.free("glu", "cdiag", "hb0", "hb1", "hsq", "mean", "var", "mean1", "var1")
        stg[0] = sb.alloc("stg0", [128, 2048], F32)
        stg[1] = sb.alloc("stg1", [128, 2048], F32)

        if upto < 2:
            ssd_chunks = 0
        xT = sb.alloc("xT", [128, 8, TP], BF16)
        BT = sb.alloc("BT", [128, 4, TP], BF16)
        CT = sb.alloc("CT", [128, 4, TP], BF16)
        dt = sb.alloc("dt", [128, NTILE, 16], F32)
        pre = [sb.alloc(f"pre{i}", [128, 3 + TP], BF16) for i in range(2)]
        wx = [sb.alloc(f"wx{i}", [128, 8, 128], BF16) for i in range(2)]
        sdiag = [sb.alloc(f"sdiag{i}", [128, 4, 128], BF16) for i in range(2)]
        wdt = sb.alloc("wdt", [128, 8, 16], BF16)
        sp_t = sb.alloc("sp_t", [128, 4, 16], F32)
        for i in range(2):
            kb.op("pool", lambda e: e.memset(pre[i][:, 0:3], 0.0), [], [f"prepad{i}"])
        bi = 0
        for xc in range(16 if upto >= 2 else 0):
            s = xc % 2
            col0 = 1024 + xc * 128
            cast_rows(wx[s], w_in_v[:, :, col0:col0 + 128], P("nmw"), f"wx{s}", 8, 128)
            for k in range(4):
                kb.op("pool", lambda e: e.tensor_scalar(out=sdiag[s][:, k, :], in0=identf, scalar1=P("scw", xc * 4 + k),
                                                        scalar2=1.0, op0=ALU.mult, op1=ALU.mult),
                      ["identf", "par"], [f"sdiag{s}"])
            for (t0, n) in blocks:
                bb = bi % 8
                bi += 1
                tkeys = [f"uT{t}" for t in range(t0 // 128, (t0 + n) // 128)]
                for dc in range(8):
                    kb.op("pe", lambda e: e.matmul(out=bk[bb][:, 0:n], lhsT=wx[s][:, dc, :], rhs=uT[:, dc, t0:t0 + n],
                                                   start=(dc == 0), stop=(dc == 7)), [f"wx{s}"] + tkeys, BK(bb))
                kb.op("act", lambda e: e.activation(out=pre[s][:, 3 + t0:3 + t0 + n], in_=bk[bb][:, 0:n], func=AF.Copy),
                      BK(bb), [f"pre{s}"])
            if xc < 8:
                dst, dkey = xT[:, xc, :], "xT"
            elif xc < 12:
                dst, dkey = BT[:, xc - 8, :], "BT"
            else:
                dst, dkey = CT[:, xc - 12, :], "CT"
            for (t0, n) in blocks:
                bb = bi % 8
                bi += 1
                for k in range(4):
                    kb.op("pe", lambda e: e.matmul(out=bk[bb][:, 0:n], lhsT=sdiag[s][:, k, :],
                                                   rhs=pre[s][:, t0 + k:t0 + k + n], start=(k == 0), stop=(k == 3)),
                          [f"sdiag{s}", f"pre{s}", f"prepad{s}"], BK(bb))
                kb.op("act", lambda e: e.activation(out=dst[:, t0:t0 + n], in_=bk[bb][:, 0:n], func=AF.Silu,
                                                    bias=P("scb", xc)), BK(bb) + ["par"],
                      [f"{dkey}{t}" for t in range(t0 // 128, (t0 + n) // 128)])
        wz = sb.alloc("wz", [128, 8, 1024], BF16)
        cast_rows(wz, w_in_v[:, :, 0:1024], P("nmw"), "wz", 8, 1024)
        DI = sb.alloc("DI", [128, 16, 128], BF16)
        for h in range(16):
            kb.op("pool", lambda e: e.tensor_scalar(out=DI[:, h, :], in0=identf, scalar1=P("Drep", h), scalar2=1.0,
                                                    op0=ALU.mult, op1=ALU.mult), ["identf", "par"], ["DI"])
        aneg = sb.alloc("aneg", [128, 16], F32)
        kb.op("act", lambda e: e.activation(out=aneg, in_=P("alog"), func=AF.Exp), ["par"], ["aneg"])
        kb.op("dve", lambda e: e.tensor_scalar(out=aneg, in0=aneg, scalar1=-1.0, scalar2=None, op0=ALU.mult),
              ["aneg"], ["aneg"])
        cast_rows(wdt, w_in_v[:, :, 3072:3088], P("nmw"), "wdt", 8, 16)
        for t in range(NTILE if upto >= 3 else 0):
            bb = bi % 8
            bi += 1
            for dc in range(8):
                kb.op("pe", lambda e: e.matmul(out=bk[bb][:, 0:16], lhsT=uT[:, dc, t * 128:(t + 1) * 128], rhs=wdt[:, dc, :],
                                               start=(dc == 0), stop=(dc == 7)), ["wdt", f"uT{t}"], BK(bb))
            xx, ax, ee, rr = sp_t[:, 0, :], sp_t[:, 1, :], sp_t[:, 2, :], sp_t[:, 3, :]
            kb.op("dve", lambda e: e.tensor_tensor(out=xx, in0=bk[bb][:, 0:16], in1=P("dtb"), op=ALU.add),
                  BK(bb) + ["par"], ["sp_x"])
            kb.op("act", lambda e: e.activation(out=ax, in_=xx, func=AF.Abs), ["sp_x"], ["sp_a"])
            kb.op("act", lambda e: e.activation(out=ee, in_=ax, func=AF.Exp, scale=-1.0), ["sp_a"], ["sp_e"])
            kb.op("act", lambda e: e.activation(out=ee, in_=ee, func=AF.Ln, bias=1.0), ["sp_e"], ["sp_e"])
            kb.op("dve", lambda e: e.tensor_scalar(out=rr, in0=xx, scalar1=0.0, scalar2=None, op0=ALU.max),
                  ["sp_x"], ["sp_r"])
            if t == 0:
                kb.op("dve", lambda e: e.tensor_tensor(out=rr, in0=rr, in1=ee, op=ALU.add), ["sp_r", "sp_e"], ["sp_r"])
                kb.op("dve", lambda e: e.tensor_scalar(out=dt[:, t, :], in0=rr, scalar1=P("dtmask", 0), scalar2=None,
                                                       op0=ALU.mult), ["sp_r", "par"], [f"dt{t}"])
            else:
                kb.op("dve", lambda e: e.tensor_tensor(out=dt[:, t, :], in0=rr, in1=ee, op=ALU.add),
                      ["sp_r", "sp_e"], [f"dt{t}"])
        sb.free("pre0", "pre1", "wx0", "wx1", "sdiag0", "sdiag1", "wdt", "sp_t")

        sb.free("stg0", "stg1")
        ynT = xT
        xtok = [sb.alloc(f"xtok{i}", [128, 1024], BF16) for i in range(2)]
        state = sb.alloc("state", [128, 16, 64], F32)
        statebf = [sb.alloc(f"statebf{i}", [128, 16, 64], BF16) for i in range(2)]
        kb.op("dve", lambda e: e.memset(state, 0.0), [], ["state"])
        kb.op("dve", lambda e: e.memset(statebf[0], 0.0), [], ["statebf0"])
        sm = {n: sb.alloc("sm_" + n, [128, NTILE, 16], F32) for n in ("dta", "ncs", "ecs", "dec", "wdec", "cd")}
        cbT = [sb.alloc(f"cbT{i}", [128, 4, 128], F32) for i in range(2)]
        NS = 4
        rhsh = [sb.alloc(f"rhsh{i}", [128, 128], F32) for i in range(NS)]
        LT = [sb.alloc(f"LT{i}", [128, 128], F32) for i in range(NS)]
        MT = [sb.alloc(f"MT{i}", [128, 128], BF16) for i in range(NS)]
        Bdec = [sb.alloc(f"Bdec{i}", [128, 128], BF16) for i in range(NS)]
        szb = sb.alloc("szb", [128, 1024], F32)
        yo = sb.alloc("yo", [128, 16, 64], F32)
        ynb = sb.alloc("ynb", [128, 1024], BF16)
        stmp = ynb.bitcast(F32).rearrange("p (h q) -> p h q", q=64)
        yb = yo.rearrange("p h q -> p (h q)")
        ssg = sb.alloc("ssg", [128, 4], F32)
        junk2 = sb.alloc("junk2", [128, 256], F32)
        btok = [sb.alloc(f"btok{i}", [128, 512], BF16) for i in range(2)]
        cst = sb.alloc("cst", [128, NTILE, 32], F32)
        hslot = 0
        CSB_BANKS = [4, 5, 6]
        ssd_deferred = []
        def ssd_decay(c):
            dta = sm["dta"][:, c, :]
            kb.op("dve", lambda e: e.tensor_tensor(out=dta, in0=dt[:, c, :], in1=aneg, op=ALU.mult),
                  [f"dt{c}", "aneg"], [f"dta{c}"])
            kb.op("pe", lambda e: e.matmul(out=bk[5][:, 0:16], lhsT=triU, rhs=dta, start=True, stop=True),
                  ["triU", f"dta{c}"], BK(5))
            kb.op("pe", lambda e: e.matmul(out=bk[5][:, 16:32], lhsT=onesf, rhs=dta, start=True, stop=True),
                  ["onesf", f"dta{c}"], BK(5))
            kb.op("dve", lambda e: e.tensor_copy(out=cst[:, c, :], in_=bk[5][:, 0:32]), BK(5), [f"cst{c}"])
            csc, tot = cst[:, c, 0:16], cst[:, c, 16:32]
            ncs, ecs, dec, wdec, cd = (sm[n][:, c, :] for n in ("ncs", "ecs", "dec", "wdec", "cd"))
            kb.op("dve", lambda e: e.tensor_scalar(out=ncs, in0=csc, scalar1=-1.0, scalar2=None, op0=ALU.mult),
                  [f"cst{c}"], [f"ncs{c}"])
            kb.op("act", lambda e: e.activation(out=ecs, in_=csc, func=AF.Exp), [f"cst{c}"], [f"ecs{c}"])
            kb.op("dve", lambda e: e.tensor_tensor(out=dec, in0=tot, in1=csc, op=ALU.subtract), [f"cst{c}"], [f"dec{c}"])
            kb.op("act", lambda e: e.activation(out=dec, in_=dec, func=AF.Exp), [f"dec{c}"], [f"dec{c}"])
            kb.op("dve", lambda e: e.tensor_tensor(out=wdec, in0=dec, in1=dt[:, c, :], op=ALU.mult),
                  [f"dec{c}", f"dt{c}"], [f"wdec{c}"])
            kb.op("act", lambda e: e.activation(out=cd, in_=tot, func=AF.Exp), [f"cst{c}"], [f"cd{c}"])

        if (ssd_chunks if upto >= 4 else 0) > 0:
            ssd_decay(0)
        for c in range(ssd_chunks if upto >= 4 else 0):
            s = c % 2
            ck = slice(c * 128, (c + 1) * 128)
            xb7 = bank_bf(7)
            for dc in range(8):
                kb.op("pe", lambda e: e.transpose(out=xb7[:, dc * 128:(dc + 1) * 128], in_=xT[:, dc, ck], identity=identb),
                      [f"xT{c}", "identb"], BK(7))
            kb.op("act", lambda e: e.activation(out=xtok[s], in_=xb7, func=AF.Copy), BK(7), [f"xtok{s}"])
            b6 = bank_bf(6)
            for g in range(4):
                kb.op("pe", lambda e: e.transpose(out=b6[:, g * 128:(g + 1) * 128], in_=BT[:, g, ck],
                                                  identity=identb), [f"BT{c}", "identb"], BK(6))
            kb.op("act", lambda e: e.activation(out=btok[s], in_=b6[:, 0:512], func=AF.Copy), BK(6), [f"btok{s}"])
            dta = sm["dta"][:, c, :]
            csc, tot = cst[:, c, 0:16], cst[:, c, 16:32]
            ncs, ecs, dec, wdec, cd = (sm[n][:, c, :] for n in ("ncs", "ecs", "dec", "wdec", "cd"))
            if c >= 1:
                for half in range(2):
                    for dc in range(8):
                        kb.op("pe", lambda e: e.matmul(out=bk[2 + half][:, :], lhsT=uT[:, dc, ck],
                                                       rhs=wz[:, dc, half * 512:(half + 1) * 512],
                                                       start=(dc == 0), stop=(dc == 7)), ["wz", f"uT{c}"], BK(2 + half))
                    kb.op("act", lambda e: e.activation(out=szb[:, half * 512:(half + 1) * 512], in_=bk[2 + half][:, :],
                                                        func=AF.Silu), BK(2 + half), [f"szb{half}"])
            for g in range(4):
                kb.op("pe", lambda e: e.matmul(out=bk[5][:, g * 128:(g + 1) * 128], lhsT=BT[:, g, ck], rhs=CT[:, g, ck],
                                               start=True, stop=True), [f"BT{c}", f"CT{c}"], BK(5))
            kb.op("act", lambda e: e.activation(out=cbT[s], in_=bk[5].rearrange("p (g n) -> p g n", n=128), func=AF.Copy),
                  BK(5), [f"cbT{s}"])
            for h in range(16):
                g = h // 4
                kb.op("pe", lambda e: e.matmul(out=bk[h // 8][:, (h % 8) * 64:(h % 8 + 1) * 64], lhsT=CT[:, g, ck],
                                               rhs=statebf[s][:, h, :], start=True, stop=True),
                      [f"CT{c}", f"statebf{s}"], BK(h // 8))
            for half in range(2):
                kb.op("dve", lambda e: e.tensor_tensor(
                    out=yo[:, half * 8:(half + 1) * 8, :], in0=bk[half].rearrange("p (h q) -> p h q", q=64),
                    in1=ecs[:, half * 8:(half + 1) * 8].unsqueeze(2).broadcast_to([128, 8, 64]), op=ALU.mult),
                    BK(half) + [f"ecs{c}"], [f"yo{half}"])
            while ssd_deferred:
                ssd_deferred.pop(0)()
            LA = 3
            hbase = hslot
            hslot += 16

            def stage1(h):
                sl = (hbase + h) % NS
                cbank = CSB_BANKS[(hbase + h) % 3]
                kb.op("dve", lambda e: e.tensor_scalar(out=rhsh[sl], in0=triU, scalar1=dta[:, h:h + 1], scalar2=None,
                                                       op0=ALU.mult), ["triU", f"dta{c}"], [f"rhsh{sl}"])
                kb.op("pe", lambda e: e.matmul(out=bk[cbank][:, 0:128], lhsT=onesf, rhs=rhsh[sl],
                                               start=True, stop=False), ["onesf", f"rhsh{sl}"], BK(cbank))
                kb.op("pe", lambda e: e.matmul(out=bk[cbank][:, 0:128], lhsT=identb, rhs=maskneg,
                                               start=False, stop=True), ["identb", "maskneg"], BK(cbank))
                kb.op("act", lambda e: e.activation(out=LT[sl], in_=bk[cbank][:, 0:128], func=AF.Exp,
                                                    bias=ncs[:, h:h + 1]), BK(cbank) + [f"ncs{c}"], [f"LT{sl}"])

            def stage2(h):
                g = h // 4
                sl = (hbase + h) % NS
                kb.op("dve", lambda e: e.scalar_tensor_tensor(out=MT[sl], in0=LT[sl], scalar=dt[:, c, h:h + 1],
                                                              in1=cbT[s][:, g, :], op0=ALU.mult, op1=ALU.mult),
                      [f"LT{sl}", f"dt{c}", f"cbT{s}"], [f"MT{sl}"])
                ysl = bk[h // 8][:, (h % 8) * 64:(h % 8 + 1) * 64]
                xh = xtok[s][:, h * 64:(h + 1) * 64]
                kb.op("pe", lambda e: e.matmul(out=ysl, lhsT=MT[sl], rhs=xh, start=True, stop=False),
                      [f"MT{sl}", f"xtok{s}"], BK(h // 8))
                kb.op("pe", lambda e: e.matmul(out=ysl, lhsT=DI[:, h, :], rhs=xh, start=False, stop=True),
                      ["DI", f"xtok{s}"], BK(h // 8))
                kb.op("dve", lambda e: e.tensor_scalar(out=Bdec[sl], in0=btok[s][:, g * 128:(g + 1) * 128],
                                                       scalar1=wdec[:, h:h + 1], scalar2=None, op0=ALU.mult),
                      [f"btok{s}", f"wdec{c}"], [f"Bdec{sl}"])
                kb.op("pe", lambda e: e.matmul(out=bk[2 + h // 8][:, (h % 8) * 64:(h % 8 + 1) * 64], lhsT=Bdec[sl], rhs=xh,
                                               start=True, stop=True), [f"Bdec{sl}", f"xtok{s}"], BK(2 + h // 8))

            for step in range(16 + LA):
                if step < 16:
                    stage1(step)
                if step >= LA:
                    stage2(step - LA)
            if c + 1 < (ssd_chunks if upto >= 4 else 0):
                ssd_decay(c + 1)
            if c < NTILE - 1:
                for half in range(2):
                    hs = slice(half * 8, (half + 1) * 8)
                    kb.op("dve", lambda e: e.tensor_tensor(out=stmp, in0=state[:, hs, :],
                                                           in1=cd[:, hs].unsqueeze(2).broadcast_to([128, 8, 64]),
                                                           op=ALU.mult), ["state", f"cd{c}"], ["ynb"])
                    kb.op("dve", lambda e: e.tensor_tensor(out=state[:, hs, :], in0=stmp,
                                                           in1=bk[2 + half].rearrange("p (h q) -> p h q", q=64),
                                                           op=ALU.add), ["ynb"] + BK(2 + half), ["state"])
                kb.op("act", lambda e: e.activation(out=statebf[1 - s], in_=state, func=AF.Copy), ["state"],
                      [f"statebf{1 - s}"])
            if c >= 1:
                for half in range(2):
                    hs = slice(half * 512, (half + 1) * 512)
                    kb.op("dve", lambda e: e.tensor_tensor(out=yb[:, hs], in0=bk[half][:, :],
                                                           in1=yo[:, half * 8:(half + 1) * 8, :].rearrange("p h q -> p (h q)"),
                                                           op=ALU.add), BK(half) + [f"yo{half}"], [f"yo{half}"])
                    kb.op("dve", lambda e: e.tensor_tensor(out=yb[:, hs], in0=yb[:, hs], in1=szb[:, hs], op=ALU.mult),
                          [f"yo{half}", f"szb{half}"], [f"yo{half}"])
                for g in range(4):
                    kb.op("act", lambda e: e.activation(out=junk2, in_=yb[:, g * 256:(g + 1) * 256], func=AF.Square,
                                                        accum_out=ssg[:, g:g + 1]), [f"yo{g // 2}"], ["junk2", "ssg"])
                rstd_from_ss(ssg, ssg, 256, ["ssg"], ["ssg"], lnexp=True)
                kb.op("dve", lambda e: e.tensor_tensor(out=ynb.rearrange("p (g n) -> p g n", n=256),
                                                       in0=yb.rearrange("p (g n) -> p g n", n=256),
                                                       in1=ssg.unsqueeze(2).broadcast_to([128, 4, 256]), op=ALU.mult),
                      ["yo0", "yo1", "ssg"], ["ynb"])
                def yn_transposes(c=c):
                    xb7_ = bank_bf(7)
                    for dc in range(8):
                        kb.op("pe", lambda e: e.transpose(out=xb7_[:, dc * 128:(dc + 1) * 128],
                                                          in_=ynb[:, dc * 128:(dc + 1) * 128], identity=identb),
                              ["ynb", "identb"], BK(7))
                    kb.op("act", lambda e: e.activation(out=ynT[:, :, (c - 1) * 128:c * 128],
                                                        in_=xb7_.rearrange("p (k n) -> p k n", n=128), func=AF.Copy),
                          BK(7), [f"xT{c - 1}"])
                ssd_deferred.append(yn_transposes)
        while ssd_deferred:
            ssd_deferred.pop(0)()
        ynT_keys = [f"xT{r}" for r in range(16)]

        run_c = upto >= 5
        for n in ["uT", "BT", "CT", "dt", "wz", "DI", "aneg", "xtok0", "xtok1", "state", "statebf0", "statebf1",
                  "sm_dta", "sm_ncs", "sm_ecs", "sm_dec", "sm_wdec", "sm_cd", "cbT0", "cbT1", "szb", "yo", "ynb", "ssg",
                  "junk2", "btok0", "btok1", "cst"] + [f"{a}{i}" for a in ("rhsh", "LT", "MT", "Bdec") for i in range(NS)]:
            del sb.blocks[n]
        kb.barrier()
        stg[0] = sb.alloc("stg0", [128, 2048], F32)
        stg[1] = sb.alloc("stg1", [128, 2048], F32)
        xnT = sb.alloc("xnT", [128, 8, SEQ], BF16)
        if run_c:
            wo = sb.alloc("wo", [128, 16, 1024], BF16)
            wq = sb.alloc("wq", [128, 8, 2048], BF16)
            w_out_v = w_out.rearrange("(k p) c -> p k c", p=128)
            cast_rows(wo[:, 0:8, :], w_out_v[:, 0:8, :], P("snw"), "wo", 8, 1024, eng="dve")
            cast_rows(wo[:, 8:16, :], w_out_v[:, 8:16, :], None, "wo", 8, 1024, eng="dve")
            xin = [sb.alloc(f"xin{i}", [128, D], F32) for i in range(2)]
            w_q_v = w_q.rearrange("(k p) c -> p k c", p=128)
            for cb_ in range(8):
                cast_rows(wq[:, :, cb_ * 256:(cb_ + 1) * 256], w_q_v[:, :, cb_ * 256:(cb_ + 1) * 256], P("nfw"), "wq", 8, 256)
            keysTb = sb.alloc("keysTb", [128, 16, 128], BF16)
            cast_rows(keysTb, keysT, None, "keysTb", 16, 128)
            h2 = xin
            ub = [sb.alloc(f"ub{i}", [128, D], BF16) for i in range(2)]
            ssC = sb.alloc("ssC", [128, 16], F32)
            bi = 0
            c_pend = []

            def c_transposes(r):
                s = r % 2
                rk = slice(r * 128, (r + 1) * 128)
                pbi = 6 + r % 2
                pb = bank_bf(pbi)
                for dc in range(8):
                    kb.op("pe", lambda e: e.transpose(out=pb[:, dc * 128:(dc + 1) * 128],
                                                      in_=ub[s][:, dc * 128:(dc + 1) * 128], identity=identb),
                          [f"ub{s}", "identb"], BK(pbi))
                kb.op("act", lambda e: e.activation(out=xnT[:, :, rk], in_=pb.rearrange("p (k n) -> p k n", n=128),
                                                    func=AF.Copy), BK(pbi), [f"xnT{r}"])

            for r in range(16):
                s = r % 2
                rk = slice(r * 128, (r + 1) * 128)
                kb.dma(xin[s], xpad[(r + 1) * 128:(r + 2) * 128, :], [], [f"xin{s}"], f"ld_xin{s}")
                for half in range(2):
                    bb = bi % 6
                    bi += 1
                    for kc in range(16):
                        lt = ynT[:, kc, rk] if kc < 8 else ycT[:, kc - 8, rk]
                        lk = f"xT{r}" if kc < 8 else f"ycT{r // 2}"
                        kb.op("pe", lambda e: e.matmul(out=bk[bb][:, :], lhsT=lt, rhs=wo[:, kc, half * 512:(half + 1) * 512],
                                                       start=(kc == 0), stop=(kc == 15)), [lk, "wo"], BK(bb))
                    kb.op("dve", lambda e: e.tensor_tensor(out=h2[s][:, half * 512:(half + 1) * 512], in0=bk[bb][:, :],
                                                           in1=xin[s][:, half * 512:(half + 1) * 512], op=ALU.add),
                          BK(bb) + [f"xin{s}"], [f"xin{s}"])
                kb.dma(h2d[rk, :], h2[s], [f"xin{s}"], [f"h2d{r}"], f"st_h2{s}")
                kb.op("act", lambda e: e.activation(out=ub[s], in_=h2[s], func=AF.Square, accum_out=ssC[:, r:r + 1]),
                      [f"xin{s}"], [f"ub{s}", f"ssC{r}"])
                rstd_from_ss(ssC[:, r:r + 1], ssC[:, r:r + 1], D, [f"ssC{r}"], [f"ssC{r}"])
                kb.op("dve", lambda e: e.tensor_scalar(out=ub[s], in0=h2[s], scalar1=ssC[:, r:r + 1], scalar2=None,
                                                       op0=ALU.mult), [f"xin{s}", f"ssC{r}"], [f"ub{s}"])
                c_pend.append(r)
                if len(c_pend) > 1:
                    c_transposes(c_pend.pop(0))
            while c_pend:
                c_transposes(c_pend.pop(0))
            for n in ["wo", "xin0", "xin1", "ub0", "ub1"]:
                del sb.blocks[n]
        for n in ["xT", "ycT"]:
            del sb.blocks[n]
        kb.barrier()

        run_d = upto >= 6
        if run_d:
            if "wq" not in sb.blocks:
                wq = sb.alloc("wq", [128, 8, 2048], BF16)
                w_q_v = w_q.rearrange("(k p) c -> p k c", p=128)
                for cb_ in range(8):
                    cast_rows(wq[:, :, cb_ * 256:(cb_ + 1) * 256], w_q_v[:, :, cb_ * 256:(cb_ + 1) * 256], P("nfw"), "wq", 8, 256)
                keysTb = sb.alloc("keysTb", [128, 16, 128], BF16)
                cast_rows(keysTb, keysT, None, "keysTb", 16, 128)
            sb.free("stg0", "stg1")
            Gs = [sb.alloc("Gs0", [128, 128, 128], BF16)] * 2
            qT = sb.alloc("qT", [128, 16, 128], BF16)
            vv = sb.alloc("vv", [128, 16, 16], F32)
            idx = sb.alloc("idx", [128, 16, 16], U32)
            idxf = sb.alloc("idxf", [128, 16, 16], F32)
            cand = sb.alloc("cand", [128, 8, 256], F32)
            scr2 = sb.alloc("scr2", [128, 8, 256], F32)
            ts = sb.alloc("ts", [128, 8, 16], F32)
            pos = sb.alloc("pos", [128, 8, 16], U32)
            apu = sb.alloc("apu", [128, 128], U32)
            bpu = sb.alloc("bpu", [128, 128], U32)
            aposf = sb.alloc("aposf", [128, 128], F32)
            bposf = sb.alloc("bposf", [128, 128], F32)
            zs = sb.alloc("zs", [128, 8], F32)
            eq = sb.alloc("eq", [128, 128, 16], F32)
            sel3 = sb.alloc("sel3", [128, 3, 128], F32)
            BUV = sb.alloc("BUV", [128, 128 * 128], BF16)
            sgnU = sb.alloc("sgnU", [128, 128], BF16)
            sgnV = sb.alloc("sgnV", [128, 128], BF16)
            sgi = sb.alloc("sgi", [128, 128], I32)
            sgf = sb.alloc("sgf", [128, 128], F32)
            pm = sb.alloc("pm", [128, 2], I32)
            pmf = sb.alloc("pmf", [128, 8], F32)
            sbU = sb.alloc("sbU", [128, 8, 128], BF16)
            sbV = sb.alloc("sbV", [128, 8, 128], BF16)
            iu = sb.alloc("iu", [128, 2, 128], U32)
            bt = [sb.alloc(f"bt{i}", [128, 128], U32) for i in range(4)]
            g2 = sb.alloc("g2", [128, 128], F32)
            NUB = 3
            Ub = [sb.alloc(f"Ub{i}", [128, 8, 128], BF16) for i in range(NUB)]
            Vb = [sb.alloc(f"Vb{i}", [128, 8, 128], BF16) for i in range(NUB)]
            kb.op("dve", lambda e: e.tensor_single_scalar(out=pm[:, 0:1], in_=pidi, scalar=31, op=ALU.bitwise_and), ["pidi"], ["pm0"])
            kb.op("dve", lambda e: e.tensor_single_scalar(out=pm[:, 1:2], in_=pidi, scalar=7, op=ALU.bitwise_and), ["pidi"], ["pm1"])
            kb.op("dve", lambda e: e.tensor_tensor(out=sgi, in0=coli, in1=pm[:, 1:2].broadcast_to([128, 128]),
                                                   op=ALU.logical_shift_right), ["coli", "pm1"], ["sgi"])
            kb.op("dve", lambda e: e.tensor_single_scalar(out=sgi, in_=sgi, scalar=1, op=ALU.bitwise_and), ["sgi"], ["sgi"])
            kb.op("dve", lambda e: e.tensor_copy(out=pmf[:, 0:2], in_=pm[:, 0:2]), ["pm0", "pm1"], ["pmf"])
            kb.op("dve", lambda e: e.tensor_scalar(out=pmf[:, 2:3], in0=pmf[:, 1:2], scalar1=7.0, scalar2=-2.0,
                                                   op0=ALU.is_equal, op1=ALU.mult), ["pmf"], ["pmf2"])
            kb.op("dve", lambda e: e.tensor_scalar(out=pmf[:, 2:3], in0=pmf[:, 2:3], scalar1=1.0, scalar2=None, op0=ALU.add),
                  ["pmf2"], ["pmf2"])
            kb.op("dve", lambda e: e.tensor_scalar(out=pmf[:, 3:4], in0=pmf[:, 0:1], scalar1=8.0, scalar2=None, op0=ALU.is_lt),
                  ["pmf"], ["pmf3"])
            kb.op("dve", lambda e: e.tensor_scalar(out=pmf[:, 4:5], in0=pmf[:, 0:1], scalar1=16.0, scalar2=None, op0=ALU.is_lt),
                  ["pmf"], ["pmf4"])
            kb.op("dve", lambda e: e.tensor_tensor(out=pmf[:, 4:5], in0=pmf[:, 4:5], in1=pmf[:, 3:4], op=ALU.subtract),
                  ["pmf4", "pmf3"], ["pmf4"])
            kb.op("dve", lambda e: e.tensor_tensor(out=pmf[:, 3:5], in0=pmf[:, 3:5], in1=pmf[:, 2:3].broadcast_to([128, 2]),
                                                   op=ALU.mult), ["pmf3", "pmf4", "pmf2"], ["pmf34"])
            kb.op("dve", lambda e: e.tensor_scalar(out=sgf, in0=sgi, scalar1=2.0, scalar2=-1.0, op0=ALU.mult, op1=ALU.add),
                  ["sgi"], ["sgf"])
            kb.op("dve", lambda e: e.tensor_scalar(out=sgnU, in0=sgf, scalar1=pmf[:, 3:4], scalar2=None, op0=ALU.mult),
                  ["sgf", "pmf34"], ["sgnU"])
            kb.op("dve", lambda e: e.tensor_scalar(out=sgnV, in0=sgf, scalar1=pmf[:, 4:5], scalar2=None, op0=ALU.mult),
                  ["sgf", "pmf34"], ["sgnV"])
            kb.op("dve", lambda e: e.memset(sbU[:, 7, :], -6.0), [], ["sbU7"])
            v4 = vv.rearrange("p (h two) k -> p h two k", two=2)
            i4 = idxf.rearrange("p (h two) k -> p h two k", two=2)
            c4 = cand.rearrange("p h (a b) -> p h a b", b=16)
            eq4 = eq.rearrange("p (h k) a -> p h k a", k=16)
            gbi = 0
            scs = sb.alloc("scs", [128, 16, 128], F32)

            def burst(tt):
                tk = slice(tt * 128, (tt + 1) * 128)
                for hh in range(16):
                    bb = hh // 4
                    for dc in range(8):
                        kb.op("pe", lambda e: e.matmul(out=bk[bb][:, (hh % 4) * 128:(hh % 4 + 1) * 128],
                                                       lhsT=wq[:, dc, hh * 128:(hh + 1) * 128], rhs=xnT[:, dc, tk],
                                                       start=(dc == 0), stop=(dc == 7)), ["wq", f"xnT{tt}"], BK(bb))
                    if hh % 4 == 3:
                        kb.op("act", lambda e: e.activation(out=qT[:, bb * 4:(bb + 1) * 4, :],
                                                            in_=bk[bb].rearrange("p (c n) -> p c n", n=128), func=AF.Copy),
                              BK(bb), [f"qT{bb}"])
                for hh in range(16):
                    bb = 4 + hh // 4
                    kb.op("pe", lambda e: e.matmul(out=bk[bb][:, (hh % 4) * 128:(hh % 4 + 1) * 128], lhsT=qT[:, hh, :],
                                                   rhs=keysTb[:, hh, :], start=True, stop=True),
                          [f"qT{hh // 4}", "keysTb"], BK(bb))
                    if hh % 4 == 3:
                        kb.op("act", lambda e: e.activation(out=scs[:, hh - 3:hh + 1, :],
                                                            in_=bk[bb].rearrange("p (c n) -> p c n", n=128), func=AF.Copy),
                              BK(bb), [f"scs{h_}" for h_ in range(hh - 3, hh + 1)])

            def prepA(tt):
                svs = [scs[:, hh, :] for hh in range(16)]
                for hh in range(16):
                    kb.op("dve", lambda e: e.max(out=vv[:, hh, 0:8], in_=svs[hh]), [f"scs{hh}"], [f"vv{hh}"])
                for hh in range(16):
                    kb.op("dve", lambda e: e.max_index(out=idx[:, hh, 0:8], in_max=vv[:, hh, 0:8], in_values=svs[hh]),
                          [f"scs{hh}", f"vv{hh}"], [f"idx{hh}"])
                for hh in range(16):
                    kb.op("dve", lambda e: e.match_replace(out=scs[:, hh, :], in_to_replace=vv[:, hh, 0:8], in_values=svs[hh],
                                                           imm_value=-1e30), [f"vv{hh}"], [f"scs{hh}"])
                for hh in range(16):
                    kb.op("dve", lambda e: e.max(out=vv[:, hh, 8:16], in_=scs[:, hh, :]), [f"scs{hh}"], [f"vv{hh}"])
                for hh in range(16):
                    kb.op("dve", lambda e: e.max_index(out=idx[:, hh, 8:16], in_max=vv[:, hh, 8:16], in_values=scs[:, hh, :]),
                          [f"scs{hh}", f"vv{hh}"], [f"idx{hh}"])
                kb.op("dve", lambda e: e.tensor_copy(out=idxf, in_=idx), [f"idx{hh}" for hh in range(16)], ["idxf"])
                kb.op("dve", lambda e: e.tensor_tensor(out=c4, in0=v4[:, :, 0, :].unsqueeze(3).broadcast_to([128, 8, 16, 16]),
                                                       in1=v4[:, :, 1, :].unsqueeze(2).broadcast_to([128, 8, 16, 16]),
                                                       op=ALU.add), [f"vv{hh}" for hh in range(16)], ["cand"])
                for h in range(8):
                    kb.op("dve", lambda e: e.max(out=ts[:, h, 0:8], in_=cand[:, h, :]), ["cand"], [f"ts{h}"])
                for h in range(8):
                    kb.op("dve", lambda e: e.max_index(out=pos[:, h, 0:8], in_max=ts[:, h, 0:8], in_values=cand[:, h, :]),
                          ["cand", f"ts{h}"], [f"pos{h}"])
                for h in range(8):
                    kb.op("dve", lambda e: e.match_replace(out=scr2[:, h, :], in_to_replace=ts[:, h, 0:8],
                                                           in_values=cand[:, h, :], imm_value=-1e30),
                          ["cand", f"ts{h}"], [f"scr2{h}"])
                for h in range(8):
                    kb.op("dve", lambda e: e.max(out=ts[:, h, 8:16], in_=scr2[:, h, :]), [f"scr2{h}"], [f"ts{h}"])
                for h in range(8):
                    kb.op("dve", lambda e: e.max_index(out=pos[:, h, 8:16], in_max=ts[:, h, 8:16], in_values=scr2[:, h, :]),
                          [f"scr2{h}", f"ts{h}"], [f"pos{h}"])
                tsk = [f"ts{h}" for h in range(8)]
                posk = [f"pos{h}" for h in range(8)]
                return

            def prepB(tt):
                tsk = [f"ts{h}" for h in range(8)]
                posk = [f"pos{h}" for h in range(8)]
                gate = sel3[:, 2, :].rearrange("p (h k) -> p h k", k=16)
                kb.op("dve", lambda e: e.tensor_tensor(out=gate, in0=ts, in1=ts[:, :, 0:1].broadcast_to([128, 8, 16]),
                                                       op=ALU.subtract), tsk, ["gate"])
                kb.op("act", lambda e: e.activation(out=gate, in_=gate, func=AF.Exp), ["gate"], ["gate"])
                kb.op("dve", lambda e: e.tensor_reduce(out=zs, in_=gate, axis=AX.X, op=ALU.add), ["gate"], ["zs"])
                kb.op("dve", lambda e: e.reciprocal(out=zs, in_=zs), ["zs"], ["zs"])
                kb.op("dve", lambda e: e.tensor_tensor(out=gate, in0=gate, in1=zs.unsqueeze(2).broadcast_to([128, 8, 16]),
                                                       op=ALU.mult), ["gate", "zs"], ["gate"])
                pu = pos.rearrange("p h k -> p (h k)")
                kb.op("dve", lambda e: e.tensor_single_scalar(out=apu, in_=pu, scalar=4, op=ALU.logical_shift_right),
                      posk, ["apu"])
                kb.op("dve", lambda e: e.tensor_single_scalar(out=bpu, in_=pu, scalar=15, op=ALU.bitwise_and),
                      posk, ["bpu"])
                kb.op("dve", lambda e: e.tensor_copy(out=aposf, in_=apu), ["apu"], ["aposf"])
                kb.op("dve", lambda e: e.tensor_copy(out=bposf, in_=bpu), ["bpu"], ["bposf"])
                for m, pp in ((0, aposf), (1, bposf)):
                    kb.op("dve", lambda e: e.tensor_tensor(out=eq, in0=pp.unsqueeze(2).broadcast_to([128, 128, 16]),
                                                           in1=colf[:, 0:16].unsqueeze(1).broadcast_to([128, 128, 16]),
                                                           op=ALU.is_equal), ["aposf", "bposf", "colf"], ["eq"])
                    kb.op("dve", lambda e: e.tensor_tensor(out=eq4, in0=eq4,
                                                           in1=i4[:, :, m, :].unsqueeze(2).broadcast_to([128, 8, 16, 16]),
                                                           op=ALU.mult), ["eq", "idxf"], ["eq"])
                    kb.op("dve", lambda e: e.tensor_reduce(out=sel3[:, m, :], in_=eq, axis=AX.X, op=ALU.add),
                          ["eq"], [f"sel{m}"])
                kb.op("dve", lambda e: e.tensor_copy(out=iu, in_=sel3[:, 0:2, :]), ["sel0", "sel1"], ["iu"])
                kb.op("dve", lambda e: e.tensor_scalar(out=g2, in0=sel3[:, 2, :], scalar1=2.0, scalar2=None, op0=ALU.mult),
                      ["gate"], ["g2"])
                kb.op("dve", lambda e: e.tensor_scalar(out=sbV[:, 7, :], in0=sel3[:, 2, :], scalar1=-6.0, scalar2=None,
                                                       op0=ALU.mult), ["gate"], ["sbV"])
                for bbit in range(7):
                    for m in range(2):
                        btt = bt[(bbit * 2 + m) % 4]
                        bkey = f"bt{(bbit * 2 + m) % 4}"
                        kb.op("dve", lambda e: e.tensor_scalar(out=btt, in0=iu[:, m, :], scalar1=bbit, scalar2=1,
                                                               op0=ALU.logical_shift_right, op1=ALU.bitwise_and),
                              ["iu"], [bkey])
                        if m == 0:
                            kb.op("dve", lambda e: e.tensor_scalar(out=sbU[:, bbit, :], in0=btt, scalar1=2.0, scalar2=-1.0,
                                                                   op0=ALU.mult, op1=ALU.add), [bkey], ["sbU"])
                        else:
                            kb.op("dve", lambda e: e.scalar_tensor_tensor(out=sbV[:, bbit, :], in0=btt, scalar=0.5, in1=g2,
                                                                          op0=ALU.subtract, op1=ALU.mult),
                                  [bkey, "g2"], ["sbV"])
                pbase = (tt % 2) * 32
                kb.dma(bitsd[tt, 0], sbU, ["sbU", "sbU7"], [f"bitsd{tt}u"], "st_bitsU")
                kb.dma(bitsd[tt, 1], sbV, ["sbV"], [f"bitsd{tt}v"], "st_bitsV")
                kb.dma(BUV[pbase:pbase + 8, :].rearrange("b (t j) -> b t j", j=128), bitsd[tt, 0].rearrange("t b j -> b t j"),
                       [f"bitsd{tt}u"], [f"BUVu{tt % 2}"], f"ld_bitsU{tt % 2}")
                kb.dma(BUV[pbase + 8:pbase + 16, :].rearrange("b (t j) -> b t j", j=128),
                       bitsd[tt, 1].rearrange("t b j -> b t j"), [f"bitsd{tt}v"], [f"BUVv{tt % 2}"], f"ld_bitsV{tt % 2}")
            def gloop(tt, mid=None):
                pbase = (tt % 2) * 32
                gs = Gs[0]
                bkeys = [f"BUVu{tt % 2}", f"BUVv{tt % 2}"]

                def uv(g):
                    par_ = g % 2
                    for tq in range(8):
                        t = g * 8 + tq
                        kb.op("pe", lambda e: e.matmul(out=qd[par_][:, tq * 128:(tq + 1) * 128],
                                                       lhsT=BUV[pbase:pbase + 16, t * 128:(t + 1) * 128], rhs=sgnU[pbase:pbase + 16, :],
                                                       start=True, stop=True), bkeys + ["sgnU"], BK(4 * par_ + tq // 4))
                    for tq in range(8):
                        t = g * 8 + tq
                        kb.op("pe", lambda e: e.matmul(out=qd[par_][:, 1024 + tq * 128:1024 + (tq + 1) * 128],
                                                       lhsT=BUV[pbase:pbase + 16, t * 128:(t + 1) * 128],
                                                       rhs=sgnV[pbase:pbase + 16, :], start=True, stop=True),
                              bkeys + ["sgnV"], BK(4 * par_ + 2 + tq // 4))
                    sl = g % NUB
                    kb.op("act", lambda e: e.activation(out=Ub[sl].rearrange("p t i -> p (t i)"), in_=qd[par_][:, 0:1024],
                                                        func=AF.Relu), BK(4 * par_) + BK(4 * par_ + 1), [f"Ub{sl}"])
                    kb.op("act", lambda e: e.activation(out=Vb[sl].rearrange("p t i -> p (t i)"), in_=qd[par_][:, 1024:2048],
                                                        func=AF.Relu), BK(4 * par_ + 2) + BK(4 * par_ + 3), [f"Vb{sl}"])

                def gmm(g):
                    par_ = g % 2
                    sl = g % NUB
                    for tq in range(8):
                        kb.op("pe", lambda e: e.matmul(out=qd[par_][:, tq * 128:(tq + 1) * 128],
                                                       lhsT=Ub[sl][:, tq, :], rhs=Vb[sl][:, tq, :], start=True, stop=True),
                              [f"Ub{sl}", f"Vb{sl}"], BK(4 * par_ + tq // 4))
                    kb.op("act", lambda e: e.activation(out=gs[:, :, g * 8:(g + 1) * 8],
                                                        in_=qd[par_][:, 0:1024].rearrange("p (t b) -> p b t", b=128),
                                                        func=AF.Copy), BK(4 * par_) + BK(4 * par_ + 1), ["Gs0"])

                uv(0)
                for g in range(16):
                    if g == 8 and mid is not None:
                        mid()
                    if g + 1 < 16:
                        uv(g + 1)
                    gmm(g)
                kb.dma(Gd[tt].rearrange("g p f -> p g f"), gs.rearrange("p (g b) t -> p g (b t)", b=8),
                       ["Gs0"], [f"Gd{tt}"], "st_G0")

            burst(0)
            prepA(0)
            prepB(0)
            for tt in range(16):
                if tt + 1 < 16:
                    burst(tt + 1)
                    prepA(tt + 1)
                    gloop(tt, mid=lambda: prepB(tt + 1))
                else:
                    gloop(tt)
            for n in (["wq", "keysTb", "Gs0", "qT", "vv", "idx", "idxf", "cand", "scr2", "ts", "pos", "apu", "bpu",
                       "aposf", "bposf", "zs", "eq", "sel3", "scs", "BUV", "sgnU", "sgnV", "sgi", "sgf", "pm", "pmf", "sbU", "sbV", "iu", "g2",
                       "bt0", "bt1", "bt2", "bt3"] + [f"Ub{i}" for i in range(NUB)] + [f"Vb{i}" for i in range(NUB)]):
                del sb.blocks[n]
            kb.barrier()

        run_e = upto >= 7
        if run_e:
            if "stg0" not in sb.blocks:
                stg[0] = sb.alloc("stg0", [128, 2048], F32)
                stg[1] = sb.alloc("stg1", [128, 2048], F32)
            yacc = sb.alloc("yacc", [128, 16, D], F32)
            wdb = [sb.alloc(f"wdb{i}", [128, 8, 1024], BF16) for i in range(2)]
            wub = [sb.alloc(f"wub{i}", [128, 8, 1024], BF16) for i in range(2)]
            Gt = [sb.alloc(f"Gt{i}", [128, 2, 1024], BF16) for i in range(2)]
            NG = 5
            gel = [sb.alloc(f"gel{i}", [128, 256], BF16) for i in range(NG)]
            At = [sb.alloc(f"At{i}", [128, 256], BF16) for i in range(NG)]
            wdT_v = wdT.rearrange("(k p) c -> p k c", p=128)
            w_up_v = w_up.rearrange("(i b) d -> i b d", b=128)

            def load_w_gen(g):
                ws = g % 2
                eng = "dve" if g == 0 else "pool"
                yield from cast_rows_gen(wdb[ws], wdT_v[:, :, g * 1024:(g + 1) * 1024], P("nfw"), f"wdb{ws}", 8, 1024, eng)
                yield from cast_rows_gen(wub[ws], w_up_v[:, g * 8:(g + 1) * 8, :], None, f"wub{ws}", 8, 1024, eng)

            def load_G(g, T):
                gsl = (g * 8 + T) % 2
                kb.dma(Gt[gsl], Gd[2 * T:2 * T + 2, g, :, :].rearrange("s p f -> p s f"),
                       [f"Gd{2 * T}", f"Gd{2 * T + 1}"], [f"Gt{gsl}"], f"ld_Gt{gsl}", q="act")

            NGRP = 16
            for _ in load_w_gen(0):
                pass
            load_G(0, 0)
            kb.dma(yacc, h2d.rearrange("(r p) d -> p r d", p=128), [f"h2d{r}" for r in range(16)], ["yacc"], "ld_yacc")
            si = 0
            DEPTH = 3
            for g in range(NGRP):
                ws = g % 2
                wgen = load_w_gen(g + 1) if g + 1 < NGRP else iter(())
                for T in range(8):
                    gsl = (g * 8 + T) % 2
                    if T + 1 < 8:
                        load_G(g, T + 1)
                    elif g + 1 < NGRP:
                        load_G(g + 1, 0)
                    for _ in range(2 if T < 7 else 99):
                        if next(wgen, "done") == "done":
                            break
                    tk = slice(T * 256, (T + 1) * 256)
                    pend = []

                    def emit_y(bb, asl):
                        for sub in range(2):
                            for dh in range(2):
                                yb_ = sub * 2 + dh
                                kb.op("pe", lambda e: e.matmul(out=bk[yb_][:, :], lhsT=At[asl][:, sub * 128:(sub + 1) * 128],
                                                               rhs=wub[ws][:, bb, dh * 512:(dh + 1) * 512],
                                                               start=(bb == 0), stop=(bb == 7)),
                                      [f"At{asl}", f"wub{ws}"], BK(yb_))

                    for bb in range(8):
                        sbk = 4 + si % 4
                        asl = si % NG
                        si += 1
                        for dc in range(8):
                            kb.op("pe", lambda e: e.matmul(out=bk[sbk][:, 0:256], lhsT=wdb[ws][:, dc, bb * 128:(bb + 1) * 128],
                                                           rhs=xnT[:, dc, tk], start=(dc == 0), stop=(dc == 7)),
                                  [f"wdb{ws}", f"xnT{2 * T}", f"xnT{2 * T + 1}"], BK(sbk))
                        kb.op("act", lambda e: e.activation(out=gel[asl], in_=bk[sbk][:, 0:256], func=AF.Gelu),
                              BK(sbk), [f"gel{asl}"])
                        kb.op("dve", lambda e: e.tensor_tensor(out=At[asl].rearrange("p (s t) -> p s t", t=128),
                                                               in0=gel[asl].rearrange("p (s t) -> p s t", t=128),
                                                               in1=Gt[gsl][:, :, bb * 128:(bb + 1) * 128], op=ALU.mult),
                              [f"gel{asl}", f"Gt{gsl}"], [f"At{asl}"])
                        pend.append((bb, asl))
                        if len(pend) > DEPTH:
                            emit_y(*pend.pop(0))
                    while pend:
                        emit_y(*pend.pop(0))
                    for sub in range(2):
                        for dh in range(2):
                            yb_ = sub * 2 + dh
                            ya = yacc[:, 2 * T + sub, dh * 512:(dh + 1) * 512]
                            kb.op("dve", lambda e: e.tensor_tensor(out=ya, in0=bk[yb_][:, :], in1=ya, op=ALU.add),
                                  BK(yb_) + [f"yacc{2 * T + sub}", "yacc"], [f"yacc{2 * T + sub}"])

            nfb = sb.alloc("nfb", [128, D], F32)
            kb.dma(nfb, nfinal.partition_broadcast(128), [], ["nfb"], "ld_nfb")
            ssF = sb.alloc("ssF", [128, 16], F32)
            junkF = gel[0]
            ot = [wdb[0].rearrange("p k c -> p (k c)").bitcast(F32)[:, 0:D], wdb[1].rearrange("p k c -> p (k c)").bitcast(F32)[:, 0:D]]
            jf = wub[0].rearrange("p k c -> p (k c)").bitcast(F32)[:, 0:D]
            for r in range(16):
                s = r % 2
                kb.op("act", lambda e: e.activation(out=jf, in_=yacc[:, r, :], func=AF.Square, accum_out=ssF[:, r:r + 1]),
                      [f"yacc{r}", "yacc"], ["wub0", f"ssF{r}"])
                rstd_from_ss(ssF[:, r:r + 1], ssF[:, r:r + 1], D, [f"ssF{r}"], [f"ssF{r}"])
                kb.op("dve", lambda e: e.scalar_tensor_tensor(out=ot[s], in0=yacc[:, r, :], scalar=ssF[:, r:r + 1], in1=nfb,
                                                              op0=ALU.mult, op1=ALU.mult),
                      [f"yacc{r}", "yacc", f"ssF{r}", "nfb"], [f"wdb{s}"])
                kb.dma(out[r * 128:(r + 1) * 128, :], ot[s], [f"wdb{s}"], [f"out{r}"], f"st_out{s}")
            out_keys = [f"out{r}" for r in range(16)]
        else:
            out_keys = []

        fin = list(out_keys)
        if "h2" in dbg:
            fin += [f"h2d{r}" for r in range(16)]
        if "G" in dbg:
            fin += [f"Gd{r}" for r in range(16)]
        if "ycT" in dbg:
            d_ = dbg_tensor("ycT", [128, 8, SEQ], BF16)
            kb.dma(d_, ycT, [f"ycT{tb}" for tb in range(8)], ["dbg_ycT"], "st_dbg")
            fin.append("dbg_ycT")
        if "ynT" in dbg:
            d_ = dbg_tensor("ynT", [128, 8, SEQ], BF16)
            kb.dma(d_, ynT[:, :, 0:SEQ], ynT_keys, ["dbg_ynT"], "st_dbg")
            fin.append("dbg_ynT")
        if "uT" in dbg:
            d_ = dbg_tensor("uT", [128, 8, TP], BF16)
            kb.dma(d_, uT, uT_keys, ["dbg_uT"], "st_dbg")
            fin.append("dbg_uT")
        kb.wait_all("sp", fin)
        print("instructions:", kb.n_ins)
    return nc, list(dbg_out.keys())


def make_inputs(inp, b):
    f = lambda a: np.ascontiguousarray(np.asarray(a, dtype=np.float32))
    x = f(inp["x"])[b]
    meta = f(inp["meta_tokens"])
    xpad = np.concatenate([np.zeros((112, D), np.float32), meta, x], axis=0)
    return xpad


def shared_inputs(inp):
    f = lambda a: np.ascontiguousarray(np.asarray(a, dtype=np.float32))
    pv = lambda v, n: f(v).reshape(n, 128).T
    par = np.zeros((128, NPAR), np.float32)

    def put(name, arr):
        o, w = PO[name]
        assert arr.shape == (128, w), (name, arr.shape)
        par[:, o:o + w] = arr

    put("nmw", pv(inp["norm_mix_w"][0], 8))
    put("nfw", pv(inp["norm_ffn_w"][0], 8))
    put("snw", pv(inp["ssd_norm_w"][0], 8))
    scw = f(inp["ssd_conv_w"][0])
    put("scw", scw.T.reshape(16, 128, 4).transpose(1, 0, 2).reshape(128, 64))
    put("scb", pv(inp["ssd_conv_b"][0], 16))
    ccw = f(inp["conf_conv_w"][0])
    put("ccw", ccw.T.reshape(8, 128, 31).transpose(1, 0, 2).reshape(128, 248))
    put("ccb", pv(inp["conf_conv_b"][0], 8))
    put("clg", pv(inp["conf_ln_g"][0], 8))
    put("clb", pv(inp["conf_ln_b"][0], 8))
    put("Drep", np.broadcast_to(f(inp["ssd_D"][0])[None, :], (128, 16)))
    put("dtb", np.broadcast_to(f(inp["ssd_dt_bias"][0])[None, :], (128, 16)))
    put("alog", np.broadcast_to(f(inp["ssd_A_log"][0])[None, :], (128, 16)))
    m = np.ones((128, 1), np.float32)
    m[:112] = 0.0
    put("dtmask", m)
    k1 = f(inp["peer_sub_keys_1"][0])
    k2 = f(inp["peer_sub_keys_2"][0])
    keys = np.stack([k1, k2], axis=1).reshape(16, 128, 128)
    keysT = np.ascontiguousarray(keys.transpose(2, 0, 1))
    wd = f(inp["peer_w_down"][0]).reshape(128, 128, D)
    wdT = np.ascontiguousarray(wd.transpose(2, 1, 0)).reshape(D, 16384)
    return {
        "w_in": f(inp["w_in"][0]), "w_out": f(inp["w_out"][0]), "w_q": f(inp["peer_w_query"][0]),
        "keysT": keysT, "wdT": wdT, "w_up": f(inp["peer_w_up"][0]), "params": par,
        "nfinal": f(inp["norm_final_w"]),
    }


_CACHE = {}


def kernel(**inputs):
    if "nc" not in _CACHE:
        _CACHE["nc"] = build_program()[0]
    nc = _CACHE["nc"]
    sh = shared_inputs(inputs)
    in_maps = []
    for b in range(8):
        m = dict(sh)
        m["xpad"] = make_inputs(inputs, b)
        in_maps.append(m)
    res = run_bass_kernel_spmd(nc, in_maps, core_ids=list(range(8)))
    return np.stack([np.asarray(r["out"], dtype=np.float32) for r in res.results], axis=0)
```
